# Optimizing a Trainium2 kernel written in Bass

```python
import math
import jax, jax.numpy as jnp
from jax import lax
import numpy as np

D_MODEL = 1024
BATCH = 16
SEQ = 2048
DEPTH = 4

N_MIXERS = 3
N_A = (DEPTH + 2) // 3
N_B = (DEPTH + 1) // 3
N_C = DEPTH // 3
EPS = 1e-6
CONV_W = 4
N_MEM = 256

M_D_INNER = 2 * D_MODEL
M_HEAD_DIM = 64
M_HEADS = M_D_INNER // M_HEAD_DIM
M_GROUPS = 8
M_STATE = 128
M_CONV_DIM = M_D_INNER + 2 * M_GROUPS * M_STATE
M_IN = M_D_INNER + M_CONV_DIM + M_HEADS
SSD_CHUNK = 64
DT_MIN = 1e-3
DT_MAX = 1e-1

H_EXPAND = 128
H_HEADS = D_MODEL // H_EXPAND
H_DV = D_MODEL // H_HEADS
HGRN_CHUNK = 32

G_HEAD_DIM = 128
G_QK_HEADS = D_MODEL // G_HEAD_DIM
G_V_HEADS = 2 * G_QK_HEADS
G_KEY_DIM = G_QK_HEADS * G_HEAD_DIM
G_VAL_DIM = G_V_HEADS * G_HEAD_DIM
G_CONV_DIM = 2 * G_KEY_DIM + G_VAL_DIM
G_IN = G_CONV_DIM + G_VAL_DIM + 2 * G_V_HEADS
GDN_CHUNK = 64

X_HEADS = 4
X_HEAD_DIM = D_MODEL // X_HEADS

D_FF = 2816
FFN_CONV_W = 3

kernel_name = 'hybrid_ssd_hgrn2_gdn_memxattn_block'


def rmsnorm(x, w):
    x32 = x.astype(jnp.float32)
    y = x32 * lax.rsqrt(jnp.mean(x32 * x32, axis=-1, keepdims=True) + EPS)
    return (y * w.astype(jnp.float32)).astype(x.dtype)


def l2norm(x):
    x32 = x.astype(jnp.float32)
    return x32 * lax.rsqrt(jnp.sum(x32 * x32, axis=-1, keepdims=True) + EPS)


def causal_dwconv(x, w):
    width, ch = w.shape
    return lax.conv_general_dilated(x, w[:, None, :].astype(x.dtype), window_strides=(1,),
                                    padding=[(width - 1, 0)],
                                    dimension_numbers=('NWC', 'WIO', 'NWC'),
                                    feature_group_count=ch)


def ssd_chunk(x, dt, a, bm, cm, chunk):
    bsz, seq, nh, p = x.shape
    ng, ns = bm.shape[2], bm.shape[3]
    r = nh // ng
    nc = seq // chunk
    f32 = jnp.float32
    xc = (x * dt[..., None]).astype(f32).reshape(bsz, nc, chunk, ng, r, p)
    acum = jnp.cumsum((dt * a).astype(f32).reshape(bsz, nc, chunk, ng, r), axis=2)
    bc = bm.astype(f32).reshape(bsz, nc, chunk, ng, ns)
    cc = cm.astype(f32).reshape(bsz, nc, chunk, ng, ns)
    causal = jnp.tril(jnp.ones((chunk, chunk), dtype=bool))[:, :, None, None]
    decay = jnp.exp(jnp.where(causal, acum[:, :, :, None] - acum[:, :, None, :], -jnp.inf))
    cb = jnp.einsum('bnlgk,bnsgk->bnlsg', cc, bc)
    y_diag = jnp.einsum('bnlsgr,bnsgrp->bnlgrp', cb[..., None] * decay, xc)

    def step(state, inp):
        c_, b_, x_, ac_ = inp
        y = jnp.einsum('blgk,bgrpk->blgrp', c_, state) * jnp.exp(ac_)[..., None]
        last = ac_[:, -1]
        ds = jnp.einsum('bsgk,bsgrp->bgrpk', b_, x_ * jnp.exp(last[:, None] - ac_)[..., None])
        state = state * jnp.exp(last)[..., None, None] + ds
        return state, y

    s0 = jnp.zeros((bsz, ng, r, p, ns), f32)
    xs = tuple(jnp.moveaxis(t, 1, 0) for t in (cc, bc, xc, acum))
    _, y_off = lax.scan(step, s0, xs)
    return (y_diag + jnp.moveaxis(y_off, 0, 1)).reshape(bsz, seq, nh, p)


def gla_chunk(q, k, v, log_f, chunk):
    bsz, seq, nh, dk = q.shape
    nc = seq // chunk

    def blocks(t):
        return t.astype(jnp.float32).reshape(bsz, nc, chunk, nh, t.shape[-1]).transpose(0, 1, 3, 2, 4)

    q, k, v = blocks(q), blocks(k), blocks(v)
    gc = jnp.cumsum(blocks(log_f), axis=3)
    g_last = gc[:, :, :, -1]
    q_dec = q * jnp.exp(gc)
    k_inv = k * jnp.exp(-gc)
    k_end = k * jnp.exp(g_last[:, :, :, None] - gc)
    causal = jnp.tril(jnp.ones((chunk, chunk), dtype=bool))
    att = jnp.where(causal, jnp.einsum('bnhlk,bnhsk->bnhls', q_dec, k_inv), 0.0)
    o_intra = jnp.einsum('bnhls,bnhsv->bnhlv', att, v)

    def step(state, inp):
        qd, ke, vv, gl = inp
        o = jnp.einsum('bhlk,bhkv->bhlv', qd, state)
        state = state * jnp.exp(gl)[..., None] + jnp.einsum('bhsk,bhsv->bhkv', ke, vv)
        return state, o

    s0 = jnp.zeros((bsz, nh, dk, v.shape[-1]), jnp.float32)
    xs = tuple(jnp.moveaxis(t, 1, 0) for t in (q_dec, k_end, v, g_last))
    _, o_inter = lax.scan(step, s0, xs)
    o = o_intra + jnp.moveaxis(o_inter, 0, 1)
    return o.transpose(0, 1, 3, 2, 4).reshape(bsz, seq, nh, -1)


def gated_delta_chunk(q, k, v, g, beta, chunk):
    bsz, seq, nh, dk = q.shape
    dv = v.shape[-1]
    nc = seq // chunk

    def blocks(t):
        return t.astype(jnp.float32).reshape(bsz, nc, chunk, nh, t.shape[-1]).transpose(0, 1, 3, 2, 4)

    def blocks_s(t):
        return t.astype(jnp.float32).reshape(bsz, nc, chunk, nh).transpose(0, 1, 3, 2)

    q, k, v = blocks(q), blocks(k), blocks(v)
    beta = blocks_s(beta)
    gc = jnp.cumsum(blocks_s(g), axis=-1)
    incl = jnp.tril(jnp.ones((chunk, chunk), dtype=bool))
    strict = jnp.tril(jnp.ones((chunk, chunk), dtype=bool), k=-1)
    decay = jnp.exp(jnp.where(incl, gc[..., :, None] - gc[..., None, :], -jnp.inf))
    kb = k * beta[..., None]
    m = jnp.where(strict, jnp.einsum('bnhlk,bnhsk->bnhls', kb, k) * decay, 0.0)
    a_mat = m + jnp.eye(chunk, dtype=jnp.float32)
    rhs = jnp.concatenate([v * beta[..., None], kb * jnp.exp(gc)[..., None]], axis=-1)
    sol = lax.linalg.triangular_solve(a_mat, rhs, left_side=True, lower=True, unit_diagonal=True)
    u, w = sol[..., :dv], sol[..., dv:]
    att = jnp.einsum('bnhlk,bnhsk->bnhls', q, k) * decay
    q_dec = q * jnp.exp(gc)[..., None]
    g_last = gc[..., -1]
    k_end = k * jnp.exp(g_last[..., None] - gc)[..., None]

    def step(state, inp):
        qd, aa, uu, ww, ke, gl = inp
        v_new = uu - jnp.einsum('bhlk,bhkv->bhlv', ww, state)
        o = jnp.einsum('bhlk,bhkv->bhlv', qd, state) + jnp.einsum('bhls,bhsv->bhlv', aa, v_new)
        state = state * jnp.exp(gl)[..., None, None] + jnp.einsum('bhsk,bhsv->bhkv', ke, v_new)
        return state, o

    s0 = jnp.zeros((bsz, nh, dk, dv), jnp.float32)
    xs = tuple(jnp.moveaxis(t, 1, 0) for t in (q_dec, att, u, w, k_end, g_last))
    _, o = lax.scan(step, s0, xs)
    o = jnp.moveaxis(o, 0, 1)
    return o.transpose(0, 1, 3, 2, 4).reshape(bsz, seq, nh, dv)


def mamba2_mixer(h, in_w, conv_w, conv_b, dt_bias, a_log, d_skip, norm_w, out_w):
    bsz, seq, _ = h.shape
    f32 = jnp.float32
    proj = h @ in_w
    z = proj[..., :M_D_INNER]
    xbc = jax.nn.silu(causal_dwconv(proj[..., M_D_INNER:M_D_INNER + M_CONV_DIM], conv_w) + conv_b)
    dt = jax.nn.softplus(proj[..., M_D_INNER + M_CONV_DIM:].astype(f32) + dt_bias.astype(f32))
    xs = xbc[..., :M_D_INNER].reshape(bsz, seq, M_HEADS, M_HEAD_DIM)
    bm = xbc[..., M_D_INNER:M_D_INNER + M_GROUPS * M_STATE].reshape(bsz, seq, M_GROUPS, M_STATE)
    cm = xbc[..., M_D_INNER + M_GROUPS * M_STATE:].reshape(bsz, seq, M_GROUPS, M_STATE)
    a = -jnp.exp(a_log.astype(f32))
    y = ssd_chunk(xs, dt, a, bm, cm, SSD_CHUNK) + d_skip.astype(f32)[:, None] * xs.astype(f32)
    y = y.reshape(bsz, seq, M_D_INNER).astype(h.dtype) * jax.nn.silu(z)
    gs = M_D_INNER // M_GROUPS
    y = rmsnorm(y.reshape(bsz, seq, M_GROUPS, gs), norm_w.reshape(M_GROUPS, gs))
    return y.reshape(bsz, seq, M_D_INNER) @ out_w


def hgrn2_mixer(h, in_w, lower_bound, norm_w, out_w):
    bsz, seq, _ = h.shape
    q, f, i, g = jnp.split(h @ in_w, 4, axis=-1)

    def heads(t):
        return t.reshape(bsz, seq, H_HEADS, -1)

    lb = lower_bound.astype(jnp.float32)
    forget = lb + (1.0 - lb) * jax.nn.sigmoid(f.astype(jnp.float32))
    o = gla_chunk(heads(jax.nn.silu(q)) * H_EXPAND ** -0.5, heads(1.0 - forget), heads(i),
                  heads(jnp.log(forget)), HGRN_CHUNK)
    o = rmsnorm(o.astype(h.dtype), norm_w) * jax.nn.silu(heads(g))
    return o.reshape(bsz, seq, D_MODEL) @ out_w


def gated_deltanet_mixer(h, in_w, conv_w, a_log, dt_bias, norm_w, out_w):
    bsz, seq, _ = h.shape
    f32 = jnp.float32
    proj = h @ in_w
    qkv = jax.nn.silu(causal_dwconv(proj[..., :G_CONV_DIM], conv_w))
    z = proj[..., G_CONV_DIM:G_CONV_DIM + G_VAL_DIM]
    b = proj[..., G_CONV_DIM + G_VAL_DIM:G_CONV_DIM + G_VAL_DIM + G_V_HEADS]
    a = proj[..., G_CONV_DIM + G_VAL_DIM + G_V_HEADS:]
    q = l2norm(qkv[..., :G_KEY_DIM].reshape(bsz, seq, G_QK_HEADS, G_HEAD_DIM))
    k = l2norm(qkv[..., G_KEY_DIM:2 * G_KEY_DIM].reshape(bsz, seq, G_QK_HEADS, G_HEAD_DIM))
    v = qkv[..., 2 * G_KEY_DIM:].reshape(bsz, seq, G_V_HEADS, G_HEAD_DIM)
    rep = G_V_HEADS // G_QK_HEADS
    q = jnp.repeat(q, rep, axis=2) * G_HEAD_DIM ** -0.5
    k = jnp.repeat(k, rep, axis=2)
    beta = jax.nn.sigmoid(b.astype(f32))
    g = -jnp.exp(a_log.astype(f32)) * jax.nn.softplus(a.astype(f32) + dt_bias.astype(f32))
    o = gated_delta_chunk(q, k, v, g, beta, GDN_CHUNK)
    o = rmsnorm(o.astype(h.dtype), norm_w) * jax.nn.silu(z.reshape(bsz, seq, G_V_HEADS, G_HEAD_DIM))
    return o.reshape(bsz, seq, G_VAL_DIM) @ out_w


def memory_cross_attention(h, mem_n, wq, wkv, wo):
    bsz, seq, _ = h.shape
    q = (h @ wq).reshape(bsz, seq, X_HEADS, X_HEAD_DIM)
    k, v = jnp.split(mem_n @ wkv, 2, axis=-1)
    k = k.reshape(bsz, -1, X_HEADS, X_HEAD_DIM)
    v = v.reshape(bsz, -1, X_HEADS, X_HEAD_DIM)
    s = jnp.einsum('blhd,bmhd->bhlm', q, k).astype(jnp.float32) * X_HEAD_DIM ** -0.5
    p = jax.nn.softmax(s, axis=-1).astype(h.dtype)
    o = jnp.einsum('bhlm,bmhd->blhd', p, v).reshape(bsz, seq, D_MODEL)
    return o @ wo


def conv_glu_ffn(h, up_w, conv_w, conv_b, down_w):
    gate, up = jnp.split(h @ up_w, 2, axis=-1)
    gate = causal_dwconv(gate, conv_w) + conv_b
    return (jax.nn.silu(gate) * up) @ down_w


def setup_inputs(seed: int = 0) -> dict:
    key = jax.random.key(seed)
    keys = iter(jax.random.split(key, 48))
    d = D_MODEL
    out_scale = (2 * DEPTH) ** -0.5

    def normal(shape, scale):
        return scale * jax.random.normal(next(keys), shape, jnp.float32)

    def gain(shape):
        return 1.0 + 0.05 * jax.random.normal(next(keys), shape, jnp.float32)

    def dt_bias(shape):
        u = jax.random.uniform(next(keys), shape, jnp.float32)
        dt = jnp.exp(u * (math.log(DT_MAX) - math.log(DT_MIN)) + math.log(DT_MIN))
        return dt + jnp.log(-jnp.expm1(-dt))

    def a_log(shape):
        return jnp.log(jax.random.uniform(next(keys), shape, jnp.float32, 1.0, 16.0))

    return {
        'x': normal((BATCH, SEQ, d), 1.0),
        'mem': normal((BATCH, N_MEM, d), 1.0),
        'ln_mix': gain((DEPTH, d)),
        'ln_xattn': gain((DEPTH, d)),
        'ln_mem': gain((DEPTH, d)),
        'ln_ffn': gain((DEPTH, d)),
        'final_norm': gain((d,)),
        'm_in_w': normal((N_A, d, M_IN), d ** -0.5),
        'm_conv_w': normal((N_A, CONV_W, M_CONV_DIM), CONV_W ** -0.5),
        'm_conv_b': normal((N_A, M_CONV_DIM), 0.02),
        'm_dt_bias': dt_bias((N_A, M_HEADS)),
        'm_a_log': a_log((N_A, M_HEADS)),
        'm_d': gain((N_A, M_HEADS)),
        'm_norm_w': gain((N_A, M_D_INNER)),
        'm_out_w': normal((N_A, M_D_INNER, d), M_D_INNER ** -0.5 * out_scale),
        'h_in_w': normal((N_B, d, 4 * d), d ** -0.5),
        'h_lower_bounds': normal((DEPTH, d), 0.1),
        'h_norm_w': gain((N_B, H_DV)),
        'h_out_w': normal((N_B, d, d), d ** -0.5 * out_scale),
        'g_in_w': normal((N_C, d, G_IN), d ** -0.5),
        'g_conv_w': normal((N_C, CONV_W, G_CONV_DIM), CONV_W ** -0.5),
        'g_a_log': a_log((N_C, G_V_HEADS)),
        'g_dt_bias': dt_bias((N_C, G_V_HEADS)),
        'g_norm_w': gain((N_C, G_HEAD_DIM)),
        'g_out_w': normal((N_C, G_VAL_DIM, d), G_VAL_DIM ** -0.5 * out_scale),
        'xa_q': normal((DEPTH, d, d), d ** -0.5),
        'xa_kv': normal((DEPTH, d, 2 * d), d ** -0.5),
        'xa_o': normal((DEPTH, d, d), d ** -0.5 * out_scale),
        'f_up': normal((DEPTH, d, 2 * D_FF), d ** -0.5),
        'f_conv_w': normal((DEPTH, FFN_CONV_W, D_FF), FFN_CONV_W ** -0.5),
        'f_conv_b': normal((DEPTH, D_FF), 0.02),
        'f_down': normal((DEPTH, D_FF, d), D_FF ** -0.5 * out_scale),
    }


def reference(x, mem, ln_mix, ln_xattn, ln_mem, ln_ffn, final_norm,
              m_in_w, m_conv_w, m_conv_b, m_dt_bias, m_a_log, m_d, m_norm_w, m_out_w,
              h_in_w, h_lower_bounds, h_norm_w, h_out_w,
              g_in_w, g_conv_w, g_a_log, g_dt_bias, g_norm_w, g_out_w,
              xa_q, xa_kv, xa_o, f_up, f_conv_w, f_conv_b, f_down):
    lb = jnp.cumsum(jax.nn.softmax(h_lower_bounds.astype(jnp.float32), axis=0), axis=0)
    lb = lb - lb[:1]
    ia = 0
    ib = 0
    ic = 0
    for i in range(DEPTH):
        hn = rmsnorm(x, ln_mix[i])
        if i % N_MIXERS == 0:
            mix = mamba2_mixer(hn, m_in_w[ia], m_conv_w[ia], m_conv_b[ia], m_dt_bias[ia],
                               m_a_log[ia], m_d[ia], m_norm_w[ia], m_out_w[ia])
            ia += 1
        elif i % N_MIXERS == 1:
            mix = hgrn2_mixer(hn, h_in_w[ib], lb[i], h_norm_w[ib], h_out_w[ib])
            ib += 1
        else:
            mix = gated_deltanet_mixer(hn, g_in_w[ic], g_conv_w[ic], g_a_log[ic], g_dt_bias[ic],
                                       g_norm_w[ic], g_out_w[ic])
            ic += 1
        x = x + mix.astype(x.dtype)
        x = x + memory_cross_attention(rmsnorm(x, ln_xattn[i]), rmsnorm(mem, ln_mem[i]),
                                       xa_q[i], xa_kv[i], xa_o[i]).astype(x.dtype)
        x = x + conv_glu_ffn(rmsnorm(x, ln_ffn[i]), f_up[i], f_conv_w[i], f_conv_b[i],
                             f_down[i]).astype(x.dtype)
    return rmsnorm(x, final_norm)
```

```python
import numpy as np
from contextlib import ExitStack
import concourse.bass as bass
import concourse.mybir as mybir
from concourse.bass_utils import run_bass_kernel_spmd

F32 = mybir.dt.float32
BF16 = mybir.dt.bfloat16
ALU = mybir.AluOpType
AF = mybir.ActivationFunctionType
AX = mybir.AxisListType

ENGS = ("pe", "act", "dve", "pool", "sp")
N_DMA_SEMS = 24


class Buf:
    __slots__ = ("name", "w", "r")

    def __init__(self, name):
        self.name = name
        self.w = None
        self.r = {}


class V:
    __slots__ = ("ap", "buf")

    def __init__(self, ap, buf):
        self.ap = ap
        self.buf = buf

    def __getitem__(self, key):
        return V(self.ap[key], self.buf)


class Tile:
    def __init__(self, t, name):
        self.t = t
        self.name = name
        self.bufs = {}

    def _buf(self, k):
        b = self.bufs.get(k)
        if b is None:
            b = self.bufs[k] = Buf(f"{self.name}.{k}")
        return b

    def __getitem__(self, key):
        return V(self.t[key], self._buf(0))

    def p(self, k):
        return _TP(self, k)


class _TP:
    def __init__(self, tile, k):
        self.tile = tile
        self.k = k

    def __getitem__(self, key):
        return V(self.tile.t[key], self.tile._buf(self.k))


class Sched:
    def __init__(self, nc, stack):
        self.nc = nc
        self.stack = stack
        self.prog = {e: [] for e in ENGS}
        self.cnt = {e: 0 for e in ENGS}
        self.waited = {e: {} for e in ENGS}
        self.sem = {}
        for e in ("pe", "act", "dve", "pool"):
            self.sem[e] = stack.enter_context(nc.semaphore("s_" + e))
        self.dsem = [stack.enter_context(nc.semaphore(f"d{i}")) for i in range(N_DMA_SEMS)]
        self.dcum = [0] * N_DMA_SEMS
        self.dnext = 0
        self.same_engine_sync = True
        self.nops = 0

    def sbuf(self, stack, name, shape, dtype):
        self.uid = getattr(self, "uid", 0) + 1
        name = f"{name}_{self.uid}"
        t = stack.enter_context(self.nc.sbuf_tensor(name, list(shape), dtype))
        return Tile(t, name)

    def psum(self, stack, name, shape, dtype):
        t = stack.enter_context(self.nc.psum_tensor(name, list(shape), dtype))
        return Tile(t, name)

    def _wait(self, eng, key, val):
        if self.waited[eng].get(key, 0) < val:
            self.waited[eng][key] = val
            self.prog[eng].append(("wait", key, val))

    def _deps(self, eng, reads, writes):
        need = {}

        def add(tok):
            key, val, peng = tok
            if peng == eng and (eng == "pe" or not self.same_engine_sync):
                return
            if need.get(key, 0) < val:
                need[key] = val

        for b in reads:
            if b.w is not None:
                add(b.w)
        for b in writes:
            if b.w is not None:
                add(b.w)
            for key, (val, peng) in b.r.items():
                add((key, val, peng))
        for key, val in need.items():
            self._wait(eng, key, val)

    def _mark(self, tok, reads, writes):
        key, val, eng = tok
        for b in reads:
            b.r[key] = (val, eng)
        for b in writes:
            b.w = tok
            b.r = {}

    def op(self, eng, fn, reads=(), writes=()):
        rb = [v.buf for v in reads]
        wb = [v.buf for v in writes]
        self._deps(eng, rb, wb)
        self.cnt[eng] += 1
        tok = (eng, self.cnt[eng], eng)
        self.prog[eng].append(("op", fn))
        self._mark(tok, rb, wb)
        self.nops += 1

    def dma(self, q, out, in_, **kw):
        rb = [in_.buf]
        wb = [out.buf]
        i = self.dnext
        self.dnext = (self.dnext + 1) % N_DMA_SEMS
        key = ("d", i)
        if self.dcum[i] > 0:
            self._wait(q, key, self.dcum[i])
        self._deps(q, rb, wb)
        self.dcum[i] += 16
        tok = (key, self.dcum[i], "dma")
        oa, ia = out.ap, in_.ap
        self.prog[q].append(("dma", (lambda e: e.dma_start(out=oa, in_=ia, **kw)), i))
        self._mark(tok, rb, wb)
        self.nops += 1

    def barrier(self):
        for e in ENGS:
            for p in ("pe", "act", "dve", "pool"):
                if p != e and self.cnt[p] > 0:
                    self._wait(e, p, self.cnt[p])
            for i in range(N_DMA_SEMS):
                if self.dcum[i] > 0:
                    self._wait(e, ("d", i), self.dcum[i])

    def finish(self):
        for i in range(N_DMA_SEMS):
            if self.dcum[i] > 0:
                self._wait("sp", ("d", i), self.dcum[i])
        for p in ("pe", "act", "dve", "pool"):
            if self.cnt[p] > 0:
                self._wait("sp", p, self.cnt[p])

    def emit(self):
        nc = self.nc

        def replay(engname, e):
            for item in self.prog[engname]:
                if item[0] == "wait":
                    key, val = item[1], item[2]
                    s = self.dsem[key[1]] if isinstance(key, tuple) else self.sem[key]
                    e.wait_ge(s, val)
                elif item[0] == "op":
                    item[1](e).then_inc(self.sem[engname], 1)
                else:
                    item[1](e).then_inc(self.dsem[item[2]], 16)

        with nc.Block() as block:
            @block.tensor
            def _(e):
                replay("pe", e)

            @block.scalar
            def _(e):
                replay("act", e)

            @block.vector
            def _(e):
                replay("dve", e)

            @block.gpsimd
            def _(e):
                replay("pool", e)

            @block.sync
            def _(e):
                replay("sp", e)

    def mm(self, out, lhsT, rhs, start, stop):
        oa, la, ra = out.ap, lhsT.ap, rhs.ap
        self.op("pe", lambda e: e.matmul(oa, la, ra, start=start, stop=stop),
                reads=[lhsT, rhs], writes=[out])

    def transpose(self, out, in_, ident):
        oa, ia, da = out.ap, in_.ap, ident.ap
        self.op("pe", lambda e: e.transpose(oa, ia, da), reads=[in_, ident], writes=[out])

    def act(self, out, in_, func, bias=None, scale=None, accum_out=None, eng="act"):
        oa, ia = out.ap, in_.ap
        reads = [in_]
        kw = {}
        if bias is not None:
            if isinstance(bias, V):
                reads.append(bias)
                kw["bias"] = bias.ap
            else:
                kw["bias"] = bias
        if scale is not None:
            if isinstance(scale, V):
                reads.append(scale)
                kw["scale"] = scale.ap
            else:
                kw["scale"] = scale
        writes = [out]
        if accum_out is not None:
            writes.append(accum_out)
            kw["accum_out"] = accum_out.ap
        self.op(eng, lambda e: e.activation(oa, ia, func, **kw), reads=reads, writes=writes)

    def tt(self, eng, out, in0, in1, op):
        oa, a, b = out.ap, in0.ap, in1.ap
        self.op(eng, lambda e: e.tensor_tensor(oa, a, b, op), reads=[in0, in1], writes=[out])

    def ts(self, eng, out, in0, s1, s2, op0, op1=None, accum_out=None):
        oa, a = out.ap, in0.ap
        reads = [in0]
        if isinstance(s1, V):
            reads.append(s1)
            s1 = s1.ap
        if isinstance(s2, V):
            reads.append(s2)
            s2 = s2.ap
        kw = {}
        if op1 is not None:
            kw["op1"] = op1
        writes = [out]
        if accum_out is not None:
            writes.append(accum_out)
            kw["accum_out"] = accum_out.ap
        self.op(eng, lambda e: e.tensor_scalar(oa, a, s1, s2, op0, **kw), reads=reads, writes=writes)

    def stt(self, out, in0, scalar, in1, op0, op1, eng="dve"):
        oa, a, b = out.ap, in0.ap, in1.ap
        reads = [in0, in1]
        if isinstance(scalar, V):
            reads.append(scalar)
            scalar = scalar.ap
        self.op(eng, lambda e: e.scalar_tensor_tensor(oa, a, scalar, b, op0, op1),
                reads=reads, writes=[out])

    def copy(self, eng, out, in_):
        oa, ia = out.ap, in_.ap
        if eng == "act":
            self.op(eng, lambda e: e.copy(oa, ia), reads=[in_], writes=[out])
        else:
            self.op(eng, lambda e: e.tensor_copy(oa, ia), reads=[in_], writes=[out])

    def memset(self, eng, out, val):
        oa = out.ap
        self.op(eng, lambda e: e.memset(oa, val), reads=[], writes=[out])

    def recip(self, out, in_):
        oa, ia = out.ap, in_.ap
        self.op("dve", lambda e: e.reciprocal(oa, ia), reads=[in_], writes=[out])


class Dram:
    def __init__(self, ap, name):
        self.ap = ap
        self.name = name
        self.bufs = {}

    def v(self, ap, k=0):
        b = self.bufs.get(k)
        if b is None:
            b = self.bufs[k] = Buf(f"{self.name}.{k}")
        return V(ap, b)

D = 1024
KD = 8
EPS = 1e-6
TB = 512
D_FF = 2816
JF = 22


class Ctx:
    def __init__(self, S, gs, consts_d):
        self.S = S
        self.banks = [S.psum(gs, f"bank{i}", [128, 512], F32) for i in range(8)]
        self.bi = 0
        self.ones_b = S.sbuf(gs, "ones_b", [128, 128], BF16)
        self.ident_f = S.sbuf(gs, "ident_f", [128, 128], F32)
        self.ident_b = S.sbuf(gs, "ident_b", [128, 128], BF16)
        S.memset("pool", self.ones_b[:], 1.0)
        S.dma("sp", self.ident_f[:], consts_d.v(consts_d.ap[0:128, 0:128]))
        S.dma("pool", self.ident_b[:], consts_d.v(consts_d.ap[0:128, 0:128]))

    def bank(self):
        b = self.banks[self.bi]
        self.bi = (self.bi + 1) % 8
        return b


def load_colvec(S, q, tile, dram, n):
    S.dma(q, tile[:], dram.v(dram.ap.rearrange("(k p) -> p k", p=128)), allow_slow_non_contiguous=True)


def rms_block(C, xb, lnw, hn, sq2, rstd, tmp, nt=TB, kd=KD, dim=D):
    S = C.S
    bk = C.bank()
    for k in range(kd):
        s = sq2[:, k % 2, 0:nt]
        S.act(s, xb[:, k, 0:nt], AF.Square)
        S.mm(bk[:, 0:nt], C.ones_b[:], s, start=(k == 0), stop=(k == kd - 1))
    S.ts("dve", tmp[:, 0:nt], bk[:, 0:nt], 1.0 / dim, EPS, ALU.mult, ALU.add)
    S.act(tmp[:, 0:nt], tmp[:, 0:nt], AF.Sqrt)
    S.recip(rstd[:, 0:nt], tmp[:, 0:nt])
    for k in range(kd):
        S.stt(hn[:, k, 0:nt], xb[:, k, 0:nt], lnw[:, k:k + 1], rstd[:, 0:nt], ALU.mult, ALU.mult)


def ffn_layer(C, x_src, x_dst, ln_d, up_d, cw_d, cb_d, down_d, NT, SEQ):
    S = C.S
    TB = 256
    with ExitStack() as ls:
        wup = S.sbuf(ls, "wup", [128, KD, 2 * D_FF], BF16)
        wdn = S.sbuf(ls, "wdn", [128, JF, D], BF16)
        lnw = S.sbuf(ls, "f_lnw", [128, KD], F32)
        cw = S.sbuf(ls, "f_cw", [128, 3, JF], F32)
        cb = S.sbuf(ls, "f_cb", [128, JF], F32)
        halo = S.sbuf(ls, "f_halo", [128, JF, 2], F32)
        xbs = [S.sbuf(ls, f"f_xb{i}", [128, KD, TB], F32) for i in range(2)]
        sq2 = S.sbuf(ls, "f_sq", [128, 2, TB], BF16)
        rstd = S.sbuf(ls, "f_rstd", [128, TB], F32)
        tmp = S.sbuf(ls, "f_tmp", [128, TB], F32)
        hn = S.sbuf(ls, "f_hn", [128, KD, TB], BF16)
        gext = [S.sbuf(ls, f"f_gext{i}", [128, TB + 2], F32) for i in range(2)]
        cv = [S.sbuf(ls, f"f_cv{i}", [128, TB], F32) for i in range(2)]
        sg = [S.sbuf(ls, f"f_sg{i}", [128, TB], F32) for i in range(2)]
        h = S.sbuf(ls, "f_h", [128, JF, TB], BF16)

        load_colvec(S, "sp", lnw, ln_d, KD)
        load_colvec(S, "sp", cb, cb_d, JF)
        S.dma("sp", cw[:], cw_d.v(cw_d.ap.rearrange("w (j p) -> p w j", p=128)), allow_slow_non_contiguous=True)
        upv = up_d.ap.rearrange("(k p) n -> p k n", p=128)
        for k in range(KD):
            S.dma("pool", wup.p(k)[:, k, :], up_d.v(upv[:, k, :]))
        dnv = down_d.ap.rearrange("(j p) n -> p j n", p=128)
        for j0 in range(0, JF, 6):
            j1 = min(JF, j0 + 6)
            S.dma("pool", wdn.p(j0)[:, j0:j1, :], down_d.v(dnv[:, j0:j1, :]))

        nblk = NT // TB
        xsv = x_src.ap.rearrange("(k p) t -> p k t", p=128)
        xdv = x_dst.ap.rearrange("(k p) t -> p k t", p=128)

        def load(i):
            S.dma("sp", xbs[i % 2][:], x_src.v(xsv[:, :, i * TB:(i + 1) * TB], i))

        load(0)
        for i in range(nblk):
            t0 = i * TB
            xb = xbs[i % 2]
            if i + 1 < nblk:
                load(i + 1)
            if t0 % SEQ == 0:
                S.memset("pool", halo[:], 0.0)
            rms_block(C, xb, lnw, hn, sq2, rstd, tmp, nt=TB)
            for j in range(JF):
                pg = C.bank()
                pu = C.bank()
                for k in range(KD):
                    S.mm(pg[:, 0:TB], wup.p(k)[:, k, j * 128:(j + 1) * 128], hn[:, k, :], start=(k == 0), stop=(k == KD - 1))
                for k in range(KD):
                    S.mm(pu[:, 0:TB], wup.p(k)[:, k, D_FF + j * 128:D_FF + (j + 1) * 128], hn[:, k, :], start=(k == 0), stop=(k == KD - 1))
                g = gext[j % 2]
                S.copy("pool", g[:, 0:2], halo[:, j, :])
                S.copy("act", g[:, 2:TB + 2], pg[:, 0:TB])
                S.copy("pool", halo[:, j, :], g[:, TB:TB + 2])
                c = cv[j % 2]
                S.ts("dve", c[:], g[:, 0:TB], cw[:, 0, j:j + 1], None, ALU.mult)
                S.stt(c[:], g[:, 1:TB + 1], cw[:, 1, j:j + 1], c[:], ALU.mult, ALU.add)
                S.stt(c[:], g[:, 2:TB + 2], cw[:, 2, j:j + 1], c[:], ALU.mult, ALU.add)
                s_ = sg[j % 2]
                S.act(s_[:], c[:], AF.Silu, bias=cb[:, j:j + 1])
                S.tt("dve", h.p(j)[:, j, :], s_[:], pu[:, 0:TB], ALU.mult)
            j0s = list(range(0, JF, 6))
            for m in range(KD):
                po = C.bank()
                for j in range(JF):
                    S.mm(po[:, 0:TB], wdn.p(j0s[j // 6])[:, j, m * 128:(m + 1) * 128], h.p(j)[:, j, :], start=(j == 0), stop=(j == JF - 1))
                S.tt("dve", xb[:, m, :], xb[:, m, :], po[:, 0:TB], ALU.add)
            S.dma("sp", x_dst.v(xdv[:, :, t0:t0 + TB], i), xb[:])
    S.barrier()


def xattn_layer(C, x_src, x_dst, mem_d, lnx_d, lnm_d, wq_d, wkv_d, wo_d, NT, SEQ, NMEM=256):
    S = C.S
    TB = 512
    with ExitStack() as ls:
        wq = S.sbuf(ls, "x_wq", [128, KD, D], BF16)
        wkv = S.sbuf(ls, "x_wkv", [128, KD, 2 * D], BF16)
        wo = S.sbuf(ls, "x_wo", [128, KD, D], BF16)
        lnx = S.sbuf(ls, "x_lnx", [128, KD], F32)
        lnm = S.sbuf(ls, "x_lnm", [128, KD], F32)
        memb = S.sbuf(ls, "x_memb", [128, KD, NMEM], F32)
        memn = S.sbuf(ls, "x_memn", [128, KD, NMEM], BF16)
        kT = S.sbuf(ls, "x_kT", [128, KD, NMEM], BF16)
        vt = S.sbuf(ls, "x_v", [128, 2, D], BF16)
        xbs = [S.sbuf(ls, f"x_xb{i}", [128, KD, TB], F32) for i in range(2)]
        sq2 = S.sbuf(ls, "x_sq", [128, 2, TB], BF16)
        rstd = S.sbuf(ls, "x_rstd", [128, TB], F32)
        tmp = S.sbuf(ls, "x_tmp", [128, TB], F32)
        hn = S.sbuf(ls, "x_hn", [128, KD, TB], BF16)
        qT = S.sbuf(ls, "x_qT", [128, KD, TB], BF16)
        pT = [S.sbuf(ls, f"x_pT{i}", [128, 2, TB], BF16) for i in range(2)]
        rs = [S.sbuf(ls, f"x_rs{i}", [128, TB], F32) for i in range(2)]
        oT = S.sbuf(ls, "x_oT", [128, KD, TB], BF16)

        load_colvec(S, "sp", lnx, lnx_d, KD)
        load_colvec(S, "sp", lnm, lnm_d, KD)
        S.dma("pool", wq[:], wq_d.v(wq_d.ap.rearrange("(k p) n -> p k n", p=128)))
        kvv = wkv_d.ap.rearrange("(k p) n -> p k n", p=128)
        S.dma("pool", wkv.p(0)[:, :, 0:D], wkv_d.v(kvv[:, :, 0:D]))
        S.dma("pool", wkv.p(1)[:, :, D:2 * D], wkv_d.v(kvv[:, :, D:2 * D]))
        S.dma("pool", wo[:], wo_d.v(wo_d.ap.rearrange("(k p) n -> p k n", p=128)))

        nblk = NT // TB
        bps = SEQ // TB
        xsv = x_src.ap.rearrange("(k p) t -> p k t", p=128)
        xdv = x_dst.ap.rearrange("(k p) t -> p k t", p=128)
        mv = mem_d.ap.rearrange("(k p) t -> p k t", p=128)

        def load(i):
            S.dma("sp", xbs[i % 2][:], x_src.v(xsv[:, :, i * TB:(i + 1) * TB], i))

        load(0)
        for i in range(nblk):
            t0 = i * TB
            xb = xbs[i % 2]
            if i + 1 < nblk:
                load(i + 1)
            if i % bps == 0:
                b = i // bps
                S.dma("sp", memb[:], mem_d.v(mv[:, :, b * NMEM:(b + 1) * NMEM], b))
                rms_block(C, memb, lnm, memn, sq2, rstd, tmp, nt=NMEM)
                for c in range(KD):
                    bk = C.bank()
                    for k in range(KD):
                        S.mm(bk[:, 0:NMEM], wkv.p(0)[:, k, c * 128:(c + 1) * 128], memn[:, k, :], start=(k == 0), stop=(k == KD - 1))
                    S.copy("act", kT[:, c, :], bk[:, 0:NMEM])
                for mt in range(2):
                    for hf in range(2):
                        bk = C.bank()
                        for k in range(KD):
                            S.mm(bk[:], memn[:, k, mt * 128:(mt + 1) * 128], wkv.p(1)[:, k, D + hf * 512:D + (hf + 1) * 512], start=(k == 0), stop=(k == KD - 1))
                        S.copy("act", vt[:, mt, hf * 512:(hf + 1) * 512], bk[:])
            rms_block(C, xb, lnx, hn, sq2, rstd, tmp, nt=TB)
            for c in range(KD):
                bk = C.bank()
                for k in range(KD):
                    S.mm(bk[:], wq[:, k, c * 128:(c + 1) * 128], hn[:, k, :], start=(k == 0), stop=(k == KD - 1))
                S.act(qT.p(c)[:, c, :], bk[:], AF.Copy, scale=1.0 / 16.0)
            for hh in range(4):
                p_ = pT[hh % 2]
                r_ = rs[hh % 2]
                for mt in range(2):
                    bk = C.bank()
                    for dc in range(2):
                        c = 2 * hh + dc
                        S.mm(bk[:], kT[:, c, mt * 128:(mt + 1) * 128], qT.p(c)[:, c, :], start=(dc == 0), stop=(dc == 1))
                    S.act(p_[:, mt, :], bk[:], AF.Exp)
                bs = C.bank()
                for mt in range(2):
                    S.mm(bs[:], C.ones_b[:], p_[:, mt, :], start=(mt == 0), stop=(mt == 1))
                S.recip(r_[:], bs[:])
                for dc in range(2):
                    c = 2 * hh + dc
                    bk = C.bank()
                    for mt in range(2):
                        S.mm(bk[:], vt[:, mt, c * 128:(c + 1) * 128], p_[:, mt, :], start=(mt == 0), stop=(mt == 1))
                    S.tt("dve", oT.p(c)[:, c, :], bk[:], r_[:], ALU.mult)
            for m in range(KD):
                po = C.bank()
                for c in range(KD):
                    S.mm(po[:], wo[:, c, m * 128:(m + 1) * 128], oT.p(c)[:, c, :], start=(c == 0), stop=(c == KD - 1))
                S.tt("dve", xb[:, m, :], xb[:, m, :], po[:], ALU.add)
            S.dma("sp", x_dst.v(xdv[:, :, t0:t0 + TB], i), xb[:])
    S.barrier()


class Rot:
    def __init__(self, items):
        self.items = items
        self.i = 0

    def __call__(self):
        b = self.items[self.i]
        self.i = (self.i + 1) % len(self.items)
        return b


def make_consts():
    c = np.zeros((128, 512), np.float32)
    c[:, 0:128] = np.eye(128)
    s = np.arange(128)[:, None]
    l = np.arange(128)[None, :]
    c[:, 128:256] = (s <= l)
    c[:, 256:384] = np.where(s <= l, 0.0, -1e6)
    c[:, 384:512] = 1.0
    return c


def load_consts(C, gs, consts_d):
    S = C.S
    C.cf = S.sbuf(gs, "cf", [128, 512], F32)
    S.dma("sp", C.cf[:], consts_d.v(consts_d.ap[:, 0:512]))
    C.triu = C.cf[:, 128:256]
    C.negmask = C.cf[:, 256:384]
    C.ones_f = C.cf[:, 384:512]
    C.idf = C.cf[:, 0:128]


def softplus_small(S, out, xin, t1, t2):
    S.stt(t1, xin, -1.0, xin, ALU.mult, ALU.max)
    S.act(t2, t1, AF.Exp, scale=-1.0)
    S.ts("dve", t2, t2, 1.0, None, ALU.add)
    S.act(t2, t2, AF.Ln)
    S.stt(out, xin, 0.0, t2, ALU.max, ALU.add)


def ssd_layer(C, x_src, x_dst, ln_d, inw_d, cw_d, cb_d, dtb_d, alog_d, dsk_d, nw_d, ow_d, NT, SEQ):
    S = C.S
    TQ = 128
    DI = 2048
    NH = 32
    with ExitStack() as ls:
        win = S.sbuf(ls, "m_win", [128, KD, 6176], BF16)
        wout = S.sbuf(ls, "m_wout", [128, 16, D], BF16)
        lnw = S.sbuf(ls, "m_lnw", [128, KD], F32)
        cw = S.sbuf(ls, "m_cw", [128, 4, 32], F32)
        cb = S.sbuf(ls, "m_cb", [128, 32], F32)
        nw = S.sbuf(ls, "m_nw", [128, 16], F32)
        dtb_r = S.sbuf(ls, "m_dtb", [128, NH], F32)
        a_r = S.sbuf(ls, "m_a", [128, NH], F32)
        dsk_r = S.sbuf(ls, "m_dsk", [128, NH], F32)
        halo = S.sbuf(ls, "m_halo", [128, 32, 3], F32)
        xbs = [S.sbuf(ls, f"m_xb{i}", [128, KD, TQ], F32) for i in range(1)]
        sq2 = S.sbuf(ls, "m_sq", [128, 2, TQ], BF16)
        rstd = S.sbuf(ls, "m_rstd", [128, 512], F32)
        tmp = S.sbuf(ls, "m_tmp", [128, 512], F32)
        hn = S.sbuf(ls, "m_hn", [128, KD, TQ], BF16)
        zs = S.sbuf(ls, "m_zs", [128, 16, TQ], BF16)
        xc = S.sbuf(ls, "m_xc", [128, 8, TQ], F32)
        Bc = S.sbuf(ls, "m_Bc", [128, 8, TQ], F32)
        Bcb = S.sbuf(ls, "m_Bcb", [128, 8, TQ], BF16)
        Ccb = S.sbuf(ls, "m_Ccb", [128, 8, TQ], BF16)
        gext = [S.sbuf(ls, f"m_gext{i}", [128, TQ + 3], F32) for i in range(2)]
        cv = [S.sbuf(ls, f"m_cv{i}", [128, TQ], F32) for i in range(2)]
        sm = S.sbuf(ls, "m_sm", [128, 10, NH], F32)
        acumT = S.sbuf(ls, "m_acumT", [32, TQ], F32)
        nacumT = S.sbuf(ls, "m_nacumT", [32, TQ], F32)
        xdt = S.sbuf(ls, "m_xdt", [128, NH, 64], BF16)
        xdtd = S.sbuf(ls, "m_xdtd", [128, NH, 64], BF16)
        xD = S.sbuf(ls, "m_xD", [128, NH, 64], BF16)
        B_tm = S.sbuf(ls, "m_Btm", [128, 8, 128], BF16)
        state = S.sbuf(ls, "m_state", [128, NH, 64], F32)
        state_b = S.sbuf(ls, "m_stateb", [128, NH, 64], BF16)
        dm = [S.sbuf(ls, f"m_dm{i}", [128, TQ], F32) for i in range(3)]
        MT = [S.sbuf(ls, f"m_MT{i}", [128, TQ], BF16) for i in range(3)]
        Ebc = [S.sbuf(ls, f"m_Ebc{i}", [128, TQ], F32) for i in range(3)]
        Cdec = [S.sbuf(ls, f"m_Cdec{i}", [128, TQ], BF16) for i in range(3)]
        yz = S.sbuf(ls, "m_yz", [128, 16, TQ], F32)
        yn = S.sbuf(ls, "m_yn", [128, 16, TQ], BF16)

        dtr, t1, t2, dtt, dA, acum, elast, dl, dtdl, t3 = [sm[:, i, :] for i in range(10)]

        load_colvec(S, "sp", lnw, ln_d, KD)
        load_colvec(S, "sp", cb, cb_d, 32)
        load_colvec(S, "sp", nw, nw_d, 16)
        S.dma("sp", cw[:], cw_d.v(cw_d.ap.rearrange("w (j p) -> p w j", p=128)), allow_slow_non_contiguous=True)
        S.dma("sp", dtb_r[:], dtb_d.v(dtb_d.ap.partition_broadcast(128)))
        S.dma("sp", a_r[:], alog_d.v(alog_d.ap.partition_broadcast(128)))
        S.dma("sp", dsk_r[:], dsk_d.v(dsk_d.ap.partition_broadcast(128)))
        S.act(a_r[:], a_r[:], AF.Exp)
        S.ts("dve", a_r[:], a_r[:], -1.0, None, ALU.mult)
        inv = inw_d.ap.rearrange("(k p) n -> p k n", p=128)
        for k in range(KD):
            S.dma("pool", win.p(k)[:, k, :], inw_d.v(inv[:, k, :]))
        owv = ow_d.ap.rearrange("(c p) n -> p c n", p=128)
        for c0 in range(0, 16, 8):
            S.dma("pool", wout.p(c0)[:, c0:c0 + 8, :], ow_d.v(owv[:, c0:c0 + 8, :]))

        ntile = NT // TQ
        xsv = x_src.ap.rearrange("(k p) t -> p k t", p=128)
        xdv = x_dst.ap.rearrange("(k p) t -> p k t", p=128)
        rot = Rot(C.banks[0:4])
        cbt = C.banks[4:6]
        hbanks = C.banks[6:8]

        def load(i):
            S.dma("sp", xbs[0][:], x_src.v(xsv[:, :, i * TQ:(i + 1) * TQ], i))

        def ind(h):
            return V(C.cf.t[0:32, h:h + 1].to_broadcast([32, 128]), C.cf._buf(0))

        load(0)
        for i in range(ntile):
            t0 = i * TQ
            xb = xbs[0]
            if i > 0:
                load(i)
            if t0 % SEQ == 0:
                S.memset("pool", halo[:], 0.0)
                S.memset("pool", state[:], 0.0)
                S.memset("pool", state_b[:], 0.0)
            rms_block(C, xb, lnw, hn, sq2, rstd, tmp, nt=TQ)
            bk = rot()
            for k in range(KD):
                S.mm(bk[:, 0:NH], hn[:, k, :], win.p(k)[:, k, 6144:6176], start=(k == 0), stop=(k == KD - 1))
            S.tt("dve", dtr, bk[:, 0:NH], dtb_r[:], ALU.add)
            softplus_small(S, dtt, dtr, t1, t2)
            S.tt("dve", dA, dtt, a_r[:], ALU.mult)
            bk = rot()
            S.mm(bk[:, 0:NH], C.triu, dA, start=True, stop=True)
            S.mm(bk[0:32, 128:256], dA, C.triu, start=True, stop=True)
            S.mm(bk[:, 256:256 + NH], C.ones_f, dA, start=True, stop=True)
            S.copy("act", acum, bk[:, 0:NH])
            S.copy("act", acumT[:], bk[0:32, 128:256])
            S.act(nacumT[:], bk[0:32, 128:256], AF.Copy, scale=-1.0)
            S.act(elast, bk[:, 256:256 + NH], AF.Exp)
            S.tt("dve", t3, bk[:, 256:256 + NH], acum, ALU.subtract)
            S.act(dl, t3, AF.Exp)
            S.tt("dve", dtdl, dl, dtt, ALU.mult)
            for q4 in range(12):
                bk = rot()
                for u in range(4):
                    ct = q4 * 4 + u
                    for k in range(KD):
                        S.mm(bk[:, u * 128:(u + 1) * 128], win.p(k)[:, k, ct * 128:(ct + 1) * 128], hn[:, k, :],
                             start=(k == 0), stop=(k == KD - 1))
                if q4 < 4:
                    S.act(zs[:, q4 * 4:(q4 + 1) * 4, :], V(bk.t[:, :].rearrange("p (a b) -> p a b", a=4), bk._buf(0)), AF.Silu)
                    continue
                for u in range(4):
                    j = (q4 - 4) * 4 + u
                    g = gext[j % 2]
                    S.copy("pool", g[:, 0:3], halo[:, j, :])
                    S.copy("act", g[:, 3:TQ + 3], bk[:, u * 128:(u + 1) * 128])
                    S.copy("pool", halo[:, j, :], g[:, TQ:TQ + 3])
                    c = cv[j % 2]
                    S.ts("dve", c[:], g[:, 0:TQ], cw[:, 0, j:j + 1], None, ALU.mult)
                    for w in range(1, 4):
                        S.stt(c[:], g[:, w:TQ + w], cw[:, w, j:j + 1], c[:], ALU.mult, ALU.add)
                    if j < 16:
                        S.act(xc.p(j % 8)[:, j % 8, :], c[:], AF.Silu, bias=cb[:, j:j + 1])
                        if j % 4 == 3:
                            tb_ = rot()
                            for u2 in range(4):
                                j2 = j - 3 + u2
                                S.transpose(tb_[:, u2 * 128:(u2 + 1) * 128], xc.p(j2 % 8)[:, j2 % 8, :], C.idf)
                            hs_ = slice((j // 4) * 8, (j // 4 + 1) * 8)
                            tbv = V(tb_.t[:, :].rearrange("p (a b) -> p a b", a=8), tb_._buf(0))

                            def bc8(v):
                                return V(v.ap[:, hs_].unsqueeze(2).to_broadcast([128, 8, 64]), v.buf)

                            S.tt("dve", xdt[:, hs_, :], tbv, bc8(dtt), ALU.mult)
                            S.tt("dve", xdtd[:, hs_, :], tbv, bc8(dtdl), ALU.mult)
                            S.tt("dve", xD[:, hs_, :], tbv, bc8(dsk_r[:]), ALU.mult)
                    elif j < 24:
                        S.act(Bc.p(j)[:, j - 16, :], c[:], AF.Silu, bias=cb[:, j:j + 1])
                        S.copy("pool", Bcb.p(j)[:, j - 16, :], Bc.p(j)[:, j - 16, :])
                    else:
                        S.act(Ccb.p(j)[:, j - 24, :], c[:], AF.Silu, bias=cb[:, j:j + 1])
            for q4 in range(2):
                bk = rot()
                for u in range(4):
                    j = q4 * 4 + u
                    S.transpose(bk[:, u * 128:(u + 1) * 128], Bc.p(16 + j)[:, j, :], C.idf)
                S.copy("act", V(B_tm.t[:, q4 * 4:(q4 + 1) * 4, :].rearrange("p a b -> p (a b)"), B_tm._buf(0)), bk[:])

            for g in range(8):
                S.mm(cbt[g // 4][:, (g % 4) * 128:(g % 4 + 1) * 128], Bcb.p(16 + g)[:, g, :], Ccb.p(24 + g)[:, g, :], start=True, stop=True)
            ybanks = [rot() for _ in range(4)]

            def stage_a(h):
                g = h // 4
                hb = hbanks[(h // 2) % 2]
                o = (h % 2) * 256
                r = h % 3
                S.mm(hb[:, o:o + 128], ind(h), acumT[:], start=True, stop=False)
                S.mm(hb[:, o:o + 128], nacumT[:], ind(h), start=False, stop=True)
                S.mm(hb[:, o + 128:o + 256], ind(h), acumT[:], start=True, stop=True)
                S.tt("dve", dm[r][:], hb[:, o:o + 128], C.negmask, ALU.add)
                S.act(dm[r][:], dm[r][:], AF.Exp)
                S.tt("dve", MT[r][:], dm[r][:], cbt[g // 4][:, (g % 4) * 128:(g % 4 + 1) * 128], ALU.mult)
                S.act(Ebc[r][:], hb[:, o + 128:o + 256], AF.Exp)
                S.tt("pool", Cdec[r][:], Ccb.p(24 + g)[:, g, :], Ebc[r][:], ALU.mult)

            def stage_b(h):
                c = h // 2
                hh = h % 2
                r = h % 3
                ybank = ybanks[c // 4]
                ycol = (c % 4) * 128
                yo = ybank.p(c)[hh * 64:(hh + 1) * 64, ycol:ycol + 128]
                S.mm(yo, xdt[:, h, :], MT[r][:], start=True, stop=False)
                S.mm(yo, state_b[:, h, :], Cdec[r][:], start=False, stop=False)
                S.mm(yo, xD[:, h, :], C.ident_b[:], start=False, stop=True)
                if hh == 1:
                    S.tt("dve", yz.p(c)[:, c, :], ybank.p(c)[:, ycol:ycol + 128], zs[:, c, :], ALU.mult)

            stage_a(0)
            for h in range(NH):
                if h + 1 < NH:
                    stage_a(h + 1)
                stage_b(h)
            for gp in range(4):
                bk = rot()
                for u in range(2):
                    g = gp * 2 + u
                    S.mm(bk[:, u * 256:(u + 1) * 256], B_tm[:, g, :],
                         V(xdtd.t[:, 4 * g:4 * g + 4, :].rearrange("p a b -> p (a b)"), xdtd._buf(0)), start=True, stop=True)
                hs = slice(gp * 8, gp * 8 + 8)
                S.tt("pool", state[:, hs, :], state[:, hs, :], V(elast.ap[:, hs].unsqueeze(2).to_broadcast([128, 8, 64]), elast.buf), ALU.mult)
                S.tt("dve", state[:, hs, :], state[:, hs, :], V(bk.t[:, :].rearrange("p (a b) -> p a b", a=8), bk._buf(0)), ALU.add)
            S.copy("act", state_b[:], state[:])
            gb = [rot(), rot()]
            for c in range(16):
                g = c // 2
                s = sq2[:, c % 2, 0:TQ]
                S.act(s, yz.p(c)[:, c, :], AF.Square)
                S.mm(gb[g // 4][:, (g % 4) * 128:(g % 4 + 1) * 128], C.ones_b[:], s, start=(c % 2 == 0), stop=(c % 2 == 1))
            rs2 = [rstd, tmp]
            for q in range(2):
                S.ts("dve", rs2[q][:], gb[q][:], 1.0 / 256.0, EPS, ALU.mult, ALU.add)
                S.act(rs2[q][:], rs2[q][:], AF.Sqrt)
                S.recip(rs2[q][:], rs2[q][:])
            for c in range(16):
                g = c // 2
                S.stt(yn.p(c)[:, c, :], yz.p(c)[:, c, :], nw[:, c:c + 1], rs2[g // 4][:, (g % 4) * 128:(g % 4 + 1) * 128], ALU.mult, ALU.mult)
            for m4 in range(2):
                bk = rot()
                for u in range(4):
                    m = m4 * 4 + u
                    for c in range(16):
                        S.mm(bk[:, u * 128:(u + 1) * 128], wout.p((c // 8) * 8)[:, c, m * 128:(m + 1) * 128], yn.p(c)[:, c, :],
                             start=(c == 0), stop=(c == 15))
                for u in range(4):
                    m = m4 * 4 + u
                    S.tt("dve", xb[:, m, :], xb[:, m, :], bk[:, u * 128:(u + 1) * 128], ALU.add)
            S.dma("sp", x_dst.v(xdv[:, :, t0:t0 + TQ], i), xb[:])
    S.barrier()


def make_consts2():
    c = np.zeros((128, 260), np.float32)
    t = np.arange(128)
    for cc in range(4):
        c[:, 256 + cc] = (t // 32 == cc)
    c[:, 0:128] = (t % 32 != 0)[None, :].astype(np.float32)
    s = t[:, None]
    l = t[None, :]
    c[:, 128:256] = ((s // 32 == l // 32) & (s <= l)).astype(np.float32)
    return c


def hgrn_layer(C, x_src, x_dst, ln_d, inw_d, hlb_d, nw_d, ow_d, consts2_d, layer_idx, depth, NT, SEQ):
    S = C.S
    TQ = 128
    NHD = 8
    QS = 128 ** -0.5
    with ExitStack() as ls:
        win = S.sbuf(ls, "h_win", [128, KD, 4 * D], BF16)
        wout = S.sbuf(ls, "h_wout", [128, KD, D], BF16)
        lnw = S.sbuf(ls, "h_lnw", [128, KD], F32)
        nw = S.sbuf(ls, "h_nw", [128, 1], F32)
        hlb = S.sbuf(ls, "h_hlb", [128, depth, KD], F32)
        lbs = S.sbuf(ls, "h_lbs", [128, 4, KD], F32)
        c2 = S.sbuf(ls, "h_c2", [128, 260], F32)
        smask = S.sbuf(ls, "h_smask", [128, NHD * TQ], F32)
        xbs = [S.sbuf(ls, f"h_xb{i}", [128, KD, TQ], F32) for i in range(2)]
        sq2 = S.sbuf(ls, "h_sq", [128, 2, TQ], BF16)
        rstd = S.sbuf(ls, "h_rstd", [128, 512], F32)
        tmp = S.sbuf(ls, "h_tmp", [128, 512], F32)
        hn = S.sbuf(ls, "h_hn", [128, KD, TQ], BF16)
        qs = S.sbuf(ls, "h_qs", [128, NHD * TQ], F32)
        sg = S.sbuf(ls, "h_sg", [128, NHD * TQ], F32)
        fg = S.sbuf(ls, "h_fg", [128, NHD * TQ], F32)
        lf = S.sbuf(ls, "h_lf", [128, NHD * TQ], F32)
        kk = S.sbuf(ls, "h_kk", [128, NHD * TQ], F32)
        gc = S.sbuf(ls, "h_gc", [128, NHD * TQ], F32)
        egc = S.sbuf(ls, "h_egc", [128, NHD * TQ], F32)
        engc = S.sbuf(ls, "h_engc", [128, NHD * TQ], F32)
        ek = S.sbuf(ls, "h_ek", [128, NHD * TQ], F32)
        kend = S.sbuf(ls, "h_kend", [128, NHD * TQ], F32)
        qdec = S.sbuf(ls, "h_qdec", [128, NHD * TQ], BF16)
        kinv = S.sbuf(ls, "h_kinv", [128, NHD * TQ], BF16)
        v_tm = S.sbuf(ls, "h_vtm", [128, NHD * 128], BF16)
        kend_tm = [S.sbuf(ls, f"h_kendtm{i}", [128, 4, 128], BF16) for i in range(2)]
        attm = [S.sbuf(ls, f"h_attm{i}", [128, 128], BF16) for i in range(2)]
        state = S.sbuf(ls, "h_state", [128, NHD, 128], F32)
        state_b0 = S.sbuf(ls, "h_stb0", [128, NHD, 128], BF16)
        stb = [S.sbuf(ls, f"h_stbr{i}", [128, 3, 128], BF16) for i in range(2)]
        o_f = S.sbuf(ls, "h_of", [128, NHD, TQ], F32)
        yt = S.sbuf(ls, "h_yt", [128, TQ], F32)
        yn = S.sbuf(ls, "h_yn", [128, NHD, TQ], BF16)

        load_colvec(S, "sp", lnw, ln_d, KD)
        S.dma("sp", nw[:], nw_d.v(nw_d.ap.rearrange("(p o) -> p o", o=1)))
        S.dma("sp", hlb[:], hlb_d.v(hlb_d.ap.rearrange("j (k p) -> p j k", p=128)), allow_slow_non_contiguous=True)
        S.dma("sp", c2[:], consts2_d.v(consts2_d.ap[:, :]))
        for hd in range(NHD):
            S.dma("sp", smask[:, hd * TQ:(hd + 1) * TQ], consts2_d.v(consts2_d.ap[:, 0:128]))
        bdmask = c2[:, 128:256]
        S.act(hlb[:], hlb[:], AF.Exp)
        S.copy("dve", lbs[:, 0, :], hlb[:, 0, :])
        for j in range(1, depth):
            S.tt("dve", lbs[:, 0, :], lbs[:, 0, :], hlb[:, j, :], ALU.add)
        S.recip(lbs[:, 0, :], lbs[:, 0, :])
        S.memset("dve", lbs[:, 3, :], 0.0)
        for j in range(1, layer_idx + 1):
            S.tt("dve", lbs[:, 3, :], lbs[:, 3, :], hlb[:, j, :], ALU.add)
        S.tt("dve", lbs[:, 1, :], lbs[:, 3, :], lbs[:, 0, :], ALU.mult)
        S.ts("dve", lbs[:, 2, :], lbs[:, 1, :], -1.0, 1.0, ALU.mult, ALU.add)
        inv = inw_d.ap.rearrange("(k p) n -> p k n", p=128)
        for k in range(KD):
            S.dma("pool", win.p(k)[:, k, :], inw_d.v(inv[:, k, :]))
        S.dma("pool", wout[:], ow_d.v(ow_d.ap.rearrange("(c p) n -> p c n", p=128)))

        ntile = NT // TQ
        xsv = x_src.ap.rearrange("(k p) t -> p k t", p=128)
        xdv = x_dst.ap.rearrange("(k p) t -> p k t", p=128)
        rot = Rot(C.banks[0:6])
        orot = Rot(C.banks[6:8])

        def load(i):
            S.dma("sp", xbs[i % 2][:], x_src.v(xsv[:, :, i * TQ:(i + 1) * TQ], i))

        load(0)
        for i in range(ntile):
            t0 = i * TQ
            xb = xbs[i % 2]
            if i + 1 < ntile:
                load(i + 1)
            if t0 % SEQ == 0:
                S.memset("pool", state[:], 0.0)
                S.memset("pool", state_b0[:], 0.0)
            rms_block(C, xb, lnw, hn, sq2, rstd, tmp, nt=TQ)
            for part, dst, fn in ((0, qs, AF.Silu), (1, fg, AF.Sigmoid), (3, sg, AF.Silu)):
                for q4 in range(2):
                    bk = rot()
                    for u in range(4):
                        ct = part * 8 + q4 * 4 + u
                        for k in range(KD):
                            S.mm(bk[:, u * 128:(u + 1) * 128], win.p(k)[:, k, ct * 128:(ct + 1) * 128], hn[:, k, :],
                                 start=(k == 0), stop=(k == KD - 1))
                    S.act(dst[:, q4 * 512:(q4 + 1) * 512], bk[:], fn)
            for hf in range(2):
                bk = rot()
                for k in range(KD):
                    S.mm(bk[:], hn[:, k, :], win.p(k)[:, k, 2 * D + hf * 512:2 * D + (hf + 1) * 512], start=(k == 0), stop=(k == KD - 1))
                S.copy("act", v_tm[:, hf * 512:(hf + 1) * 512], bk[:])
            for hd in range(NHD):
                sl = slice(hd * TQ, (hd + 1) * TQ)
                S.ts("dve", fg[:, sl], fg[:, sl], lbs[:, 2, hd:hd + 1], lbs[:, 1, hd:hd + 1], ALU.mult, ALU.add)
            S.act(lf[:], fg[:], AF.Ln)
            S.ts("pool", kk[:], fg[:], -1.0, 1.0, ALU.mult, ALU.add)
            lfa, sma, gca = lf[:].ap, smask[:].ap, gc[:].ap
            S.op("dve", lambda e, lfa=lfa, sma=sma, gca=gca: e.tensor_tensor_scan(gca, sma, lfa, 0.0, ALU.mult, ALU.add),
                 reads=[lf[:], smask[:]], writes=[gc[:]])
            S.act(egc[:], gc[:], AF.Exp)
            S.act(engc[:], gc[:], AF.Exp, scale=-1.0)
            gc3 = V(gc.t[:, :].rearrange("p (c q) -> p c q", q=32), gc._buf(0))
            gl3 = V(gc.t[:, :].rearrange("p (c q) -> p c q", q=32)[:, :, 31:32].to_broadcast([128, NHD * 4, 32]), gc._buf(0))
            ek3 = V(ek.t[:, :].rearrange("p (c q) -> p c q", q=32), ek._buf(0))
            S.tt("dve", ek3, gl3, gc3, ALU.subtract)
            S.act(ek[:], ek[:], AF.Exp)
            S.stt(qdec[:], qs[:], QS, egc[:], ALU.mult, ALU.mult)
            S.tt("pool", kinv[:], kk[:], engc[:], ALU.mult)
            S.tt("pool", kend[:], kk[:], ek[:], ALU.mult)
            for hd in range(NHD):
                sl = slice(hd * TQ, (hd + 1) * TQ)
                r = hd % 2
                bk = rot()
                S.transpose(bk[:, 0:128], kend[:, sl], C.idf)
                for c in range(4):
                    S.act(kend_tm[r][:, c, :], bk[:, 0:128], AF.Copy, scale=c2[:, 256 + c:257 + c])
                S.mm(bk[:, 128:256], kinv[:, sl], qdec[:, sl], start=True, stop=True)
                S.tt("dve", attm[r][:], bk[:, 128:256], bdmask, ALU.mult)
                dsb = rot()
                for c in range(4):
                    S.mm(dsb[:, c * 128:(c + 1) * 128], kend_tm[r][:, c, :], v_tm[:, hd * 128:(hd + 1) * 128], start=True, stop=True)
                sts = [state_b0[:, hd, :]] + [stb[r][:, c, :] for c in range(3)]
                for c in range(4):
                    col = hd * TQ + 32 * c + 31
                    S.stt(state[:, hd, :], state[:, hd, :], egc[:, col:col + 1], dsb[:, c * 128:(c + 1) * 128], ALU.mult, ALU.add)
                    if c < 3:
                        S.copy("act", stb[r][:, c, :], state[:, hd, :])
                ob = orot()
                oo = ob[:, 0:128]
                S.mm(oo, v_tm[:, hd * 128:(hd + 1) * 128], attm[r][:], start=True, stop=False)
                for c in range(4):
                    S.mm(ob[:, c * 32:(c + 1) * 32], sts[c], qdec[:, hd * TQ + 32 * c:hd * TQ + 32 * c + 32], start=False, stop=(c == 3))
                S.copy("act", state_b0[:, hd, :], state[:, hd, :])
                S.copy("act", o_f[:, hd, :], oo)
            gb = [rot(), rot()]
            for hd in range(NHD):
                s = sq2[:, hd % 2, 0:TQ]
                S.act(s, o_f[:, hd, :], AF.Square)
                S.mm(gb[hd // 4][:, (hd % 4) * 128:(hd % 4 + 1) * 128], C.ones_b[:], s, start=True, stop=True)
            rs2 = [rstd, tmp]
            for q in range(2):
                S.ts("dve", rs2[q][:], gb[q][:], 1.0 / 128.0, EPS, ALU.mult, ALU.add)
                S.act(rs2[q][:], rs2[q][:], AF.Sqrt)
                S.recip(rs2[q][:], rs2[q][:])
            for hd in range(NHD):
                S.stt(yt[:], o_f[:, hd, :], nw[:, 0:1], rs2[hd // 4][:, (hd % 4) * 128:(hd % 4 + 1) * 128], ALU.mult, ALU.mult)
                S.tt("dve", yn.p(hd)[:, hd, :], yt[:], sg[:, hd * TQ:(hd + 1) * TQ], ALU.mult)
            for m4 in range(2):
                bk = rot()
                for u in range(4):
                    m = m4 * 4 + u
                    for c in range(NHD):
                        S.mm(bk[:, u * 128:(u + 1) * 128], wout[:, c, m * 128:(m + 1) * 128], yn.p(c)[:, c, :],
                             start=(c == 0), stop=(c == NHD - 1))
                for u in range(4):
                    m = m4 * 4 + u
                    S.tt("dve", xb[:, m, :], xb[:, m, :], bk[:, u * 128:(u + 1) * 128], ALU.add)
            S.dma("sp", x_dst.v(xdv[:, :, t0:t0 + TQ], i), xb[:])
    S.barrier()


def make_consts3():
    c = np.zeros((128, 900), np.float32)
    t = np.arange(128)
    s = t[:, None]
    l = t[None, :]
    same = (s // 64 == l // 64)
    c[:, 0:128] = (same & (s <= l))
    c[:, 128:256] = same
    c[:, 256:384] = (s < 64) & (l >= 0)
    c[:, 384:512] = (s >= 64) & (l >= 0)
    c[:, 512:640] = np.where(same & (s <= l), 0.0, -1e6)
    c[:, 640:768] = np.where(same & (s < l), 0.0, -1e6)
    c[:, 768:896] = np.where(same & (l < s), 0.0, -1e6)
    c[:, 896] = (t < 64)
    c[:, 897] = (t >= 64)
    return c


def gdn_layer(C, x_src, x_dst, ln_d, inw_d, cw_d, alog_d, dtb_d, nw_d, ow_d, consts3_d, NT, SEQ):
    S = C.S
    TQ = 128
    NV = 16
    NQ = 8
    QS = 128 ** -0.5
    with ExitStack() as ls:
        win = S.sbuf(ls, "g_win", [128, KD, 6176], BF16)
        wout = S.sbuf(ls, "g_wout", [128, 16, D], BF16)
        lnw = S.sbuf(ls, "g_lnw", [128, KD], F32)
        cw = S.sbuf(ls, "g_cw", [128, 4, 32], F32)
        nw = S.sbuf(ls, "g_nw", [128, 1], F32)
        dtb_r = S.sbuf(ls, "g_dtb", [128, NV], F32)
        nea_r = S.sbuf(ls, "g_nea", [128, NV], F32)
        c3 = S.sbuf(ls, "g_c3", [128, 900], F32)
        halo = S.sbuf(ls, "g_halo", [128, 32, 3], F32)
        xbs = [S.sbuf(ls, f"g_xb{i}", [128, KD, TQ], F32) for i in range(1)]
        sq2 = S.sbuf(ls, "g_sq", [128, 2, TQ], BF16)
        rstd = S.sbuf(ls, "g_rstd", [128, 512], F32)
        tmp = S.sbuf(ls, "g_tmp", [128, 256], F32)
        hn = S.sbuf(ls, "g_hn", [128, KD, TQ], BF16)
        zs = S.sbuf(ls, "g_zs", [128, 16, TQ], BF16)
        gext = [S.sbuf(ls, f"g_gext{i}", [128, TQ + 3], F32) for i in range(2)]
        cv = [S.sbuf(ls, f"g_cv{i}", [128, TQ], F32) for i in range(2)]
        qc = S.sbuf(ls, "g_qc", [128, 8, TQ], F32)
        qn = S.sbuf(ls, "g_qn", [128, 8, TQ], BF16)
        knb = S.sbuf(ls, "g_knb", [128, 8, TQ], BF16)
        k_tm = S.sbuf(ls, "g_ktm", [128, 8, 128], F32)
        vc = S.sbuf(ls, "g_vc", [128, 4, TQ], F32)
        v_tm = S.sbuf(ls, "g_vtm", [128, NV, 128], F32)
        sm = S.sbuf(ls, "g_sm", [128, 20, NV], F32)
        gcT = S.sbuf(ls, "g_gcT", [16, TQ], F32)
        ngcT = S.sbuf(ls, "g_ngcT", [16, TQ], F32)
        gbT = S.sbuf(ls, "g_gbT", [16, TQ], F32)
        state = S.sbuf(ls, "g_state", [128, NV, 128], F32)
        state_b = S.sbuf(ls, "g_stateb", [128, NV, 128], BF16)
        E1 = [S.sbuf(ls, f"g_E1{e}", [128, 128], F32) for e in range(2)]
        Ebc = [S.sbuf(ls, f"g_Ebc{e}", [128, 128], F32) for e in range(2)]
        PM = [[S.sbuf(ls, f"g_PM{e}{i}", [128, 128], F32) for i in range(2)] for e in range(2)]
        PN = [[S.sbuf(ls, f"g_PN{e}{i}", [128, 128], F32) for i in range(2)] for e in range(2)]
        Rm = [S.sbuf(ls, f"g_R{e}", [128, 128], F32) for e in range(2)]
        attT = [S.sbuf(ls, f"g_attT{e}", [128, 128], BF16) for e in range(2)]
        TTb = [S.sbuf(ls, f"g_TTb{e}", [128, 128], BF16) for e in range(2)]
        qdec = [S.sbuf(ls, f"g_qdec{e}", [128, 128], BF16) for e in range(2)]
        kbg = [S.sbuf(ls, f"g_kbg{e}", [128, 128], BF16) for e in range(2)]
        vb = [S.sbuf(ls, f"g_vb{e}", [128, 128], BF16) for e in range(2)]
        kend = [[S.sbuf(ls, f"g_kend{e}{c}", [128, 128], BF16) for c in range(2)] for e in range(2)]
        u_f = [S.sbuf(ls, f"g_uf{e}", [128, 128], F32) for e in range(2)]
        wTb = [S.sbuf(ls, f"g_wTb{e}", [128, 128], BF16) for e in range(2)]
        vnew = [S.sbuf(ls, f"g_vnew{e}", [128, 128], BF16) for e in range(2)]
        S1b = [S.sbuf(ls, f"g_S1b{e}", [128, 128], BF16) for e in range(2)]
        o_f = [S.sbuf(ls, f"g_of{e}", [128, 128], F32) for e in range(2)]
        yt = S.sbuf(ls, "g_yt", [128, TQ], F32)
        yn = S.sbuf(ls, "g_yn", [128, NV, TQ], BF16)

        (b_, beta, lnbeta, ar, t1, t2, sp, g_, gc, gb, tot, egl0, egl1, ke, ke0, ke1, bg, t3) = [sm[:, i, :] for i in range(18)]

        load_colvec(S, "sp", lnw, ln_d, KD)
        S.dma("sp", nw[:], nw_d.v(nw_d.ap.rearrange("(p o) -> p o", o=1)))
        S.dma("sp", cw[:], cw_d.v(cw_d.ap.rearrange("w (j p) -> p w j", p=128)), allow_slow_non_contiguous=True)
        S.dma("sp", dtb_r[:], dtb_d.v(dtb_d.ap.partition_broadcast(128)))
        S.dma("sp", nea_r[:], alog_d.v(alog_d.ap.partition_broadcast(128)))
        S.dma("sp", c3[:], consts3_d.v(consts3_d.ap[:, :]))
        S.act(nea_r[:], nea_r[:], AF.Exp)
        S.ts("dve", nea_r[:], nea_r[:], -1.0, None, ALU.mult)
        for e in range(2):
            S.memset("pool", vnew[e][:], 0.0)
        inv = inw_d.ap.rearrange("(k p) n -> p k n", p=128)
        for k in range(KD):
            S.dma("pool", win.p(k)[:, k, :], inw_d.v(inv[:, k, :]))
        owv = ow_d.ap.rearrange("(c p) n -> p c n", p=128)
        for c0 in range(0, 16, 8):
            S.dma("pool", wout.p(c0)[:, c0:c0 + 8, :], ow_d.v(owv[:, c0:c0 + 8, :]))
        tri64 = c3[:, 0:128]
        bd = c3[:, 128:256]
        cm0 = c3[:, 256:384]
        cm1 = c3[:, 384:512]
        mask_i = c3[:, 512:640]
        mask_u = c3[:, 640:768]
        mask_l = c3[:, 768:896]

        ntile = NT // TQ
        xsv = x_src.ap.rearrange("(k p) t -> p k t", p=128)
        xdv = x_dst.ap.rearrange("(k p) t -> p k t", p=128)
        bankA = C.banks[0]
        bankP = C.banks[1:3]
        bankQ = C.banks[3:5]
        rot = Rot(C.banks[5:8])

        def ind(h):
            return V(C.cf.t[0:16, h:h + 1].to_broadcast([16, 128]), C.cf._buf(0))

        def l2norm_group(is_q):
            src = qc
            for q4 in range(2):
                nb = rot()
                for u in range(4):
                    hq = q4 * 4 + u
                    s = sq2[:, hq % 2, 0:TQ]
                    S.act(s, src.p(hq)[:, hq, :], AF.Square)
                    S.mm(nb[:, u * 128:(u + 1) * 128], C.ones_b[:], s, start=True, stop=True)
                S.ts("dve", rstd[:], nb[:], 1.0, EPS, ALU.mult, ALU.add)
                S.act(rstd[:], rstd[:], AF.Sqrt)
                S.recip(rstd[:], rstd[:])
                for u in range(4):
                    hq = q4 * 4 + u
                    if is_q:
                        S.stt(qn.p(hq)[:, hq, :], src.p(hq)[:, hq, :], QS, rstd[:, u * 128:(u + 1) * 128], ALU.mult, ALU.mult)
                    else:
                        S.tt("dve", src.p(hq)[:, hq, :], src.p(hq)[:, hq, :], rstd[:, u * 128:(u + 1) * 128], ALU.mult)
                        S.copy("pool", knb.p(hq)[:, hq, :], src.p(hq)[:, hq, :])
            if not is_q:
                for q4 in range(2):
                    tb_ = rot()
                    for u in range(4):
                        hq = q4 * 4 + u
                        S.transpose(tb_[:, u * 128:(u + 1) * 128], src.p(hq)[:, hq, :], C.idf)
                    S.copy("act", V(k_tm.t[:, q4 * 4:(q4 + 1) * 4, :].rearrange("p a b -> p (a b)"), k_tm._buf(0)), tb_[:])

        for i in range(ntile):
            t0 = i * TQ
            xb = xbs[0]
            S.dma("sp", xb[:], x_src.v(xsv[:, :, t0:t0 + TQ], i))
            if t0 % SEQ == 0:
                S.memset("pool", halo[:], 0.0)
                S.memset("pool", state[:], 0.0)
                S.memset("pool", state_b[:], 0.0)
            rms_block(C, xb, lnw, hn, sq2, rstd, tmp, nt=TQ)
            bk = rot()
            for k in range(KD):
                S.mm(bk[:, 0:32], hn[:, k, :], win.p(k)[:, k, 6144:6176], start=(k == 0), stop=(k == KD - 1))
            S.copy("act", b_, bk[:, 0:16])
            S.tt("dve", ar, bk[:, 16:32], dtb_r[:], ALU.add)
            S.act(beta, b_, AF.Sigmoid)
            S.act(lnbeta, beta, AF.Ln)
            softplus_small(S, sp, ar, t1, t2)
            S.tt("dve", g_, sp, nea_r[:], ALU.mult)
            bk = rot()
            S.mm(bk[:, 0:16], tri64, g_, start=True, stop=True)
            S.mm(bk[0:16, 128:256], g_, tri64, start=True, stop=True)
            S.mm(bk[:, 256:272], bd, g_, start=True, stop=True)
            S.mm(bk[:, 272:288], cm0, g_, start=True, stop=True)
            S.mm(bk[:, 288:304], cm1, g_, start=True, stop=True)
            S.copy("act", gc, bk[:, 0:16])
            S.copy("act", gcT[:], bk[0:16, 128:256])
            S.act(ngcT[:], bk[0:16, 128:256], AF.Copy, scale=-1.0)
            S.act(egl0, bk[:, 272:288], AF.Exp)
            S.act(egl1, bk[:, 288:304], AF.Exp)
            S.tt("dve", t3, bk[:, 256:272], gc, ALU.subtract)
            S.act(ke, t3, AF.Exp)
            S.ts("dve", ke0, ke, c3[:, 896:897], None, ALU.mult)
            S.ts("dve", ke1, ke, c3[:, 897:898], None, ALU.mult)
            S.act(t3, gc, AF.Exp)
            S.tt("dve", bg, t3, beta, ALU.mult)
            S.tt("dve", gb, gc, lnbeta, ALU.add)
            bk = rot()
            S.mm(bk[0:16, 0:128], gb, C.idf, start=True, stop=True)
            S.copy("act", gbT[:], bk[0:16, 0:128])
            for q4 in range(12):
                bk = rot()
                for u in range(4):
                    ct = q4 * 4 + u
                    for k in range(KD):
                        S.mm(bk[:, u * 128:(u + 1) * 128], win.p(k)[:, k, ct * 128:(ct + 1) * 128], hn[:, k, :],
                             start=(k == 0), stop=(k == KD - 1))
                if q4 >= 8:
                    z4 = q4 - 8
                    S.act(zs[:, z4 * 4:(z4 + 1) * 4, :], V(bk.t[:, :].rearrange("p (a b) -> p a b", a=4), bk._buf(0)), AF.Silu)
                    continue
                for u in range(4):
                    j = q4 * 4 + u
                    g = gext[j % 2]
                    S.copy("pool", g[:, 0:3], halo[:, j, :])
                    S.copy("act", g[:, 3:TQ + 3], bk[:, u * 128:(u + 1) * 128])
                    S.copy("pool", halo[:, j, :], g[:, TQ:TQ + 3])
                    c = cv[j % 2]
                    S.ts("dve", c[:], g[:, 0:TQ], cw[:, 0, j:j + 1], None, ALU.mult)
                    for w in range(1, 4):
                        S.stt(c[:], g[:, w:TQ + w], cw[:, w, j:j + 1], c[:], ALU.mult, ALU.add)
                    if j < 16:
                        S.act(qc.p(j % 8)[:, j % 8, :], c[:], AF.Silu)
                        if j % 8 == 7:
                            l2norm_group(j < 8)
                    else:
                        jv = j - 16
                        S.act(vc.p(jv % 4)[:, jv % 4, :], c[:], AF.Silu)
                        if jv % 4 == 3:
                            tb_ = rot()
                            for u2 in range(4):
                                j2 = jv - 3 + u2
                                S.transpose(tb_[:, u2 * 128:(u2 + 1) * 128], vc.p(j2 % 4)[:, j2 % 4, :], C.idf)
                            S.copy("act", V(v_tm.t[:, jv - 3:jv + 1, :].rearrange("p a b -> p (a b)"), v_tm._buf(0)), tb_[:])
            for p in range(NQ):
                hq = p
                S.mm(bankA[:, 0:128], knb.p(hq)[:, hq, :], knb.p(hq)[:, hq, :], start=True, stop=True)
                S.mm(bankA[:, 128:256], knb.p(hq)[:, hq, :], qn.p(hq)[:, hq, :], start=True, stop=True)
                cur = [0, 0]
                for e in range(2):
                    hv = 2 * p + e
                    P = bankP[e]
                    S.mm(P[:, 0:128], ind(hv), gcT[:], start=True, stop=True)
                    S.mm(P[:, 128:256], ind(hv), gbT[:], start=True, stop=True)
                    S.mm(P[:, 256:384], ind(hv), ngcT[:], start=True, stop=True)
                    pm, pn = PM[e][0], PN[e][0]
                    S.stt(E1[e][:], P[:, 0:128], gc[:, hv:hv + 1], mask_i, ALU.subtract, ALU.add)
                    S.act(E1[e][:], E1[e][:], AF.Exp)
                    S.stt(pn[:], P[:, 128:256], gc[:, hv:hv + 1], mask_u, ALU.subtract, ALU.add)
                    S.act(pn[:], pn[:], AF.Exp)
                    S.stt(pm[:], P[:, 256:384], gb[:, hv:hv + 1], mask_l, ALU.add, ALU.add)
                    S.act(pm[:], pm[:], AF.Exp)
                    S.act(Ebc[e][:], P[:, 0:128], AF.Exp)
                    S.tt("dve", attT[e][:], E1[e][:], bankA[:, 128:256], ALU.mult)
                    S.tt("dve", pn[:], pn[:], bankA[:, 0:128], ALU.mult)
                    S.tt("dve", pm[:], pm[:], bankA[:, 0:128], ALU.mult)
                    S.stt(Rm[e][:], pn[:], -1.0, C.idf, ALU.mult, ALU.add)
                    S.tt("pool", qdec[e][:], qn.p(hq)[:, hq, :], Ebc[e][:], ALU.mult)
                    S.act(kbg[e][:], k_tm[:, hq, :], AF.Copy, scale=bg[:, hv:hv + 1])
                    S.act(vb[e][:], v_tm[:, hv, :], AF.Copy, scale=beta[:, hv:hv + 1])
                    S.act(kend[e][0][:], k_tm[:, hq, :], AF.Copy, scale=ke0[:, hv:hv + 1])
                    S.act(kend[e][1][:], k_tm[:, hq, :], AF.Copy, scale=ke1[:, hv:hv + 1])
                for lev in range(1, 6):
                    for e in range(2):
                        P = bankP[e]
                        a = cur[e]
                        pm, pn = PM[e][a], PN[e][a]
                        pm2, pn2 = PM[e][1 - a], PN[e][1 - a]
                        S.mm(P[:, 0:128], pn[:], pm[:], start=True, stop=True)
                        if lev < 5:
                            S.mm(P[:, 128:256], pm[:], pn[:], start=True, stop=True)
                        S.copy("act", pm2[:], P[:, 0:128])
                        if lev < 5:
                            S.copy("act", pn2[:], P[:, 128:256])
                        S.mm(P[:, 256:384], pm2[:], Rm[e][:], start=True, stop=True)
                        S.tt("dve", Rm[e][:], Rm[e][:], P[:, 256:384], ALU.add)
                        cur[e] = 1 - a
                for e in range(2):
                    hv = 2 * p + e
                    Q = bankQ[e]
                    P = bankP[e]
                    S.copy("act", TTb[e][:], Rm[e][:])
                    S.mm(Q[:, 0:128], TTb[e][:], vb[e][:], start=True, stop=True)
                    S.mm(Q[:, 128:256], kbg[e][:], TTb[e][:], start=True, stop=True)
                    S.copy("act", u_f[e][:], Q[:, 0:128])
                    S.copy("act", wTb[e][:], Q[:, 128:256])
                    S.mm(Q[0:64, 256:384], wTb[e][:, 0:64], state_b[:, hv, :], start=True, stop=True)
                    S.tt("dve", vnew[e][0:64, :], u_f[e][0:64, :], Q[0:64, 256:384], ALU.subtract)
                    S.mm(Q[:, 384:512], kend[e][0][:], vnew[e][:], start=True, stop=True)
                    S.stt(state[:, hv, :], state[:, hv, :], egl0[:, hv:hv + 1], Q[:, 384:512], ALU.mult, ALU.add)
                    S.copy("act", S1b[e][:], state[:, hv, :])
                    S.mm(Q[64:128, 256:384], wTb[e][:, 64:128], S1b[e][:], start=True, stop=True)
                    S.tt("dve", vnew[e][64:128, :], u_f[e][64:128, :], Q[64:128, 256:384], ALU.subtract)
                    S.mm(Q[:, 384:512], kend[e][1][:], vnew[e][:], start=True, stop=True)
                    S.stt(state[:, hv, :], state[:, hv, :], egl1[:, hv:hv + 1], Q[:, 384:512], ALU.mult, ALU.add)
                    S.mm(P[:, 0:128], vnew[e][:], attT[e][:], start=True, stop=False)
                    S.mm(P[:, 0:64], state_b[:, hv, :], qdec[e][:, 0:64], start=False, stop=False)
                    S.mm(P[:, 64:128], S1b[e][:], qdec[e][:, 64:128], start=False, stop=True)
                    S.copy("act", state_b[:, hv, :], state[:, hv, :])
                    S.copy("act", o_f[e][:], P[:, 0:128])
                for e in range(2):
                    s = sq2[:, e, 0:TQ]
                    S.act(s, o_f[e][:], AF.Square)
                    S.mm(bankA[:, 256 + e * 128:256 + (e + 1) * 128], C.ones_b[:], s, start=True, stop=True)
                S.ts("dve", tmp[:, 0:256], bankA[:, 256:512], 1.0 / 128.0, EPS, ALU.mult, ALU.add)
                S.act(tmp[:, 0:256], tmp[:, 0:256], AF.Sqrt)
                S.recip(tmp[:, 0:256], tmp[:, 0:256])
                for e in range(2):
                    hv = 2 * p + e
                    S.stt(yt[:], o_f[e][:], nw[:, 0:1], tmp[:, e * 128:(e + 1) * 128], ALU.mult, ALU.mult)
                    S.tt("dve", yn.p(hv)[:, hv, :], yt[:], zs[:, hv, :], ALU.mult)
            for m4 in range(2):
                bk = rot()
                for u in range(4):
                    m = m4 * 4 + u
                    for c in range(16):
                        S.mm(bk[:, u * 128:(u + 1) * 128], wout.p((c // 8) * 8)[:, c, m * 128:(m + 1) * 128], yn.p(c)[:, c, :],
                             start=(c == 0), stop=(c == 15))
                for u in range(4):
                    m = m4 * 4 + u
                    S.tt("dve", xb[:, m, :], xb[:, m, :], bk[:, u * 128:(u + 1) * 128], ALU.add)
            S.dma("sp", x_dst.v(xdv[:, :, t0:t0 + TQ], i), xb[:])
    S.barrier()


def final_norm_layer(C, x_src, out_d, ln_d, NT):
    S = C.S
    TB = 512
    with ExitStack() as ls:
        lnw = S.sbuf(ls, "n_lnw", [128, KD], F32)
        xbs = [S.sbuf(ls, f"n_xb{i}", [128, KD, TB], F32) for i in range(2)]
        sq2 = S.sbuf(ls, "n_sq", [128, 2, TB], BF16)
        rstd = S.sbuf(ls, "n_rstd", [128, TB], F32)
        tmp = S.sbuf(ls, "n_tmp", [128, TB], F32)
        load_colvec(S, "sp", lnw, ln_d, KD)
        nblk = NT // TB
        xsv = x_src.ap.rearrange("(k p) t -> p k t", p=128)
        xdv = out_d.ap.rearrange("(k p) t -> p k t", p=128)
        for i in range(nblk):
            xb = xbs[i % 2]
            S.dma("sp", xb[:], x_src.v(xsv[:, :, i * TB:(i + 1) * TB], i))
            rms_block(C, xb, lnw, xb, sq2, rstd, tmp, nt=TB)
            S.dma("sp", out_d.v(xdv[:, :, i * TB:(i + 1) * TB], i), xb[:])
    S.barrier()


DEPTH = 4
NSEQ_CORE = 2
SEQ_LEN = 2048
NMEM = 256
N_CORES = 8

W_SHAPES = {
    "ln_mix": (4, 1024), "ln_xattn": (4, 1024), "ln_mem": (4, 1024), "ln_ffn": (4, 1024), "final_norm": (1024,),
    "m_in_w": (2, 1024, 6176), "m_conv_w": (2, 4, 4096), "m_conv_b": (2, 4096), "m_dt_bias": (2, 32), "m_a_log": (2, 32),
    "m_d": (2, 32), "m_norm_w": (2, 2048), "m_out_w": (2, 2048, 1024),
    "h_in_w": (1, 1024, 4096), "h_lower_bounds": (4, 1024), "h_norm_w": (1, 128), "h_out_w": (1, 1024, 1024),
    "g_in_w": (1, 1024, 6176), "g_conv_w": (1, 4, 4096), "g_a_log": (1, 16), "g_dt_bias": (1, 16), "g_norm_w": (1, 128),
    "g_out_w": (1, 2048, 1024),
    "xa_q": (4, 1024, 1024), "xa_kv": (4, 1024, 2048), "xa_o": (4, 1024, 1024),
    "f_up": (4, 1024, 5632), "f_conv_w": (4, 3, 2816), "f_conv_b": (4, 2816), "f_down": (4, 2816, 1024),
}


def build_program(nseq=NSEQ_CORE, seq=SEQ_LEN, depth=DEPTH):
    NT = nseq * seq
    nc = bass.Bass("TRN2", target_bir_lowering=False)

    def din(name, shape):
        return Dram(nc.dram_tensor(name, list(shape), F32, kind="ExternalInput").ap(), name)

    xT = din("xT", [D, NT])
    memT = din("memT", [D, nseq * NMEM])
    consts = din("consts", [128, 512])
    consts2 = din("consts2", [128, 260])
    consts3 = din("consts3", [128, 900])
    W = {k: din(k, s) for k, s in W_SHAPES.items()}
    outT = Dram(nc.dram_tensor("outT", [D, NT], F32, kind="ExternalOutput").ap(), "outT")
    res = Dram(nc.dram_tensor("res", [D, NT], F32).ap(), "res")

    def sub(name, idx):
        return Dram(W[name].ap[idx], f"{name}[{idx}]")

    with ExitStack() as gs:
        S = Sched(nc, gs)
        C = Ctx(S, gs, consts)
        load_consts(C, gs, consts)
        S.barrier()
        ia = ib = ic = 0
        src = xT
        for i in range(depth):
            if i % 3 == 0:
                ssd_layer(C, src, res, sub("ln_mix", i), sub("m_in_w", ia), sub("m_conv_w", ia), sub("m_conv_b", ia),
                          sub("m_dt_bias", ia), sub("m_a_log", ia), sub("m_d", ia), sub("m_norm_w", ia), sub("m_out_w", ia), NT, seq)
                ia += 1
            elif i % 3 == 1:
                hgrn_layer(C, src, res, sub("ln_mix", i), sub("h_in_w", ib), W["h_lower_bounds"], sub("h_norm_w", ib),
                           sub("h_out_w", ib), consts2, i, depth, NT, seq)
                ib += 1
            else:
                gdn_layer(C, src, res, sub("ln_mix", i), sub("g_in_w", ic), sub("g_conv_w", ic), sub("g_a_log", ic),
                          sub("g_dt_bias", ic), sub("g_norm_w", ic), sub("g_out_w", ic), consts3, NT, seq)
                ic += 1
            src = res
            xattn_layer(C, res, res, memT, sub("ln_xattn", i), sub("ln_mem", i), sub("xa_q", i), sub("xa_kv", i), sub("xa_o", i), NT, seq)
            ffn_layer(C, res, res, sub("ln_ffn", i), sub("f_up", i), sub("f_conv_w", i), sub("f_conv_b", i), sub("f_down", i), NT, seq)
        final_norm_layer(C, res, outT, W["final_norm"], NT)
        S.finish()
        S.emit()
    return nc


_NC_CACHE = {}


def kernel(**inputs):
    x = np.asarray(inputs["x"], dtype=np.float32)
    mem = np.asarray(inputs["mem"], dtype=np.float32)
    B, L, _ = x.shape
    nseq = B // N_CORES
    if "nc" not in _NC_CACHE:
        _NC_CACHE["nc"] = build_program(nseq, L, DEPTH)
    nc = _NC_CACHE["nc"]
    shared = {k: np.ascontiguousarray(np.asarray(inputs[k], dtype=np.float32)) for k in W_SHAPES}
    shared["consts"] = make_consts()
    shared["consts2"] = make_consts2()
    shared["consts3"] = make_consts3()
    in_maps = []
    for c in range(N_CORES):
        m = dict(shared)
        m["xT"] = np.ascontiguousarray(x[c * nseq:(c + 1) * nseq].reshape(nseq * L, D).T)
        m["memT"] = np.ascontiguousarray(mem[c * nseq:(c + 1) * nseq].reshape(nseq * NMEM, D).T)
        in_maps.append(m)
    res = run_bass_kernel_spmd(nc, in_maps, core_ids=list(range(N_CORES)))
    out = np.empty((B, L, D), np.float32)
    for c in range(N_CORES):
        out[c * nseq:(c + 1) * nseq] = res.results[c]["outT"].T.reshape(nseq, L, D)
    return out
```

```python
import numpy as np
from contextlib import ExitStack
import concourse.bass as bass
import concourse.mybir as mybir
from concourse.bass_utils import run_bass_kernel_spmd

F32 = mybir.dt.float32
BF16 = mybir.dt.bfloat16
ALU = mybir.AluOpType
AF = mybir.ActivationFunctionType
AX = mybir.AxisListType

ENGS = ("pe", "act", "dve", "pool", "sp")
N_DMA_SEMS = 24


class Buf:
    __slots__ = ("name", "w", "r")

    def __init__(self, name):
        self.name = name
        self.w = None
        self.r = {}


class V:
    __slots__ = ("ap", "buf")

    def __init__(self, ap, buf):
        self.ap = ap
        self.buf = buf

    def __getitem__(self, key):
        return V(self.ap[key], self.buf)


class Tile:
    def __init__(self, t, name):
        self.t = t
        self.name = name
        self.bufs = {}

    def _buf(self, k):
        b = self.bufs.get(k)
        if b is None:
            b = self.bufs[k] = Buf(f"{self.name}.{k}")
        return b

    def __getitem__(self, key):
        return V(self.t[key], self._buf(0))

    def p(self, k):
        return _TP(self, k)


class _TP:
    def __init__(self, tile, k):
        self.tile = tile
        self.k = k

    def __getitem__(self, key):
        return V(self.tile.t[key], self.tile._buf(self.k))


class Sched:
    def __init__(self, nc, stack):
        self.nc = nc
        self.stack = stack
        self.prog = {e: [] for e in ENGS}
        self.cnt = {e: 0 for e in ENGS}
        self.waited = {e: {} for e in ENGS}
        self.sem = {}
        for e in ("pe", "act", "dve", "pool"):
            self.sem[e] = stack.enter_context(nc.semaphore("s_" + e))
        self.dsem = [stack.enter_context(nc.semaphore(f"d{i}")) for i in range(N_DMA_SEMS)]
        self.dcum = [0] * N_DMA_SEMS
        self.dnext = 0
        self.same_engine_sync = True
        self.inline_waits = True
        self.nops = 0

    def sbuf(self, stack, name, shape, dtype):
        self.uid = getattr(self, "uid", 0) + 1
        name = f"{name}_{self.uid}"
        t = stack.enter_context(self.nc.sbuf_tensor(name, list(shape), dtype))
        return Tile(t, name)

    def psum(self, stack, name, shape, dtype):
        t = stack.enter_context(self.nc.psum_tensor(name, list(shape), dtype))
        return Tile(t, name)

    def _wait(self, eng, key, val):
        if self.waited[eng].get(key, 0) < val:
            self.waited[eng][key] = val
            self.prog[eng].append(("wait", key, val))

    def _deps(self, eng, reads, writes):
        need = {}

        def add(tok):
            key, val, peng = tok
            if peng == eng and (eng == "pe" or not self.same_engine_sync):
                return
            if need.get(key, 0) < val:
                need[key] = val

        for b in reads:
            if b.w is not None:
                add(b.w)
        for b in writes:
            if b.w is not None:
                add(b.w)
            for key, (val, peng) in b.r.items():
                add((key, val, peng))
        for key, val in need.items():
            self._wait(eng, key, val)

    def _mark(self, tok, reads, writes):
        key, val, eng = tok
        for b in reads:
            b.r[key] = (val, eng)
        for b in writes:
            b.w = tok
            b.r = {}

    def op(self, eng, fn, reads=(), writes=()):
        rb = [v.buf for v in reads]
        wb = [v.buf for v in writes]
        self._deps(eng, rb, wb)
        self.cnt[eng] += 1
        tok = (eng, self.cnt[eng], eng)
        self.prog[eng].append(("op", fn))
        self._mark(tok, rb, wb)
        self.nops += 1

    def dma(self, q, out, in_, **kw):
        rb = [in_.buf]
        wb = [out.buf]
        i = self.dnext
        self.dnext = (self.dnext + 1) % N_DMA_SEMS
        key = ("d", i)
        if self.dcum[i] > 0:
            self._wait(q, key, self.dcum[i])
        self._deps(q, rb, wb)
        self.dcum[i] += 16
        tok = (key, self.dcum[i], "dma")
        oa, ia = out.ap, in_.ap
        self.prog[q].append(("dma", (lambda e: e.dma_start(out=oa, in_=ia, **kw)), i))
        self._mark(tok, rb, wb)
        self.nops += 1

    def barrier(self):
        for e in ENGS:
            for p in ("pe", "act", "dve", "pool"):
                if p != e and self.cnt[p] > 0:
                    self._wait(e, p, self.cnt[p])
            for i in range(N_DMA_SEMS):
                if self.dcum[i] > 0:
                    self._wait(e, ("d", i), self.dcum[i])

    def finish(self):
        for i in range(N_DMA_SEMS):
            if self.dcum[i] > 0:
                self._wait("sp", ("d", i), self.dcum[i])
        for p in ("pe", "act", "dve", "pool"):
            if self.cnt[p] > 0:
                self._wait("sp", p, self.cnt[p])

    def emit(self):
        nc = self.nc

        def replay(engname, e):
            pend = []
            for item in self.prog[engname]:
                if item[0] == "wait":
                    pend.append(item)
                    continue
                inline = None
                if pend and item[0] == "op" and self.inline_waits:
                    inline = pend.pop()
                for w in pend:
                    key, val = w[1], w[2]
                    e.wait_ge(self.dsem[key[1]] if isinstance(key, tuple) else self.sem[key], val)
                pend = []
                if item[0] == "op":
                    ins = item[1](e)
                    if inline is not None:
                        key, val = inline[1], inline[2]
                        ins._wait_ge(self.dsem[key[1]] if isinstance(key, tuple) else self.sem[key], val)
                    ins.then_inc(self.sem[engname], 1)
                else:
                    item[1](e).then_inc(self.dsem[item[2]], 16)
            for w in pend:
                key, val = w[1], w[2]
                e.wait_ge(self.dsem[key[1]] if isinstance(key, tuple) else self.sem[key], val)

        with nc.Block() as block:
            @block.tensor
            def _(e):
                replay("pe", e)

            @block.scalar
            def _(e):
                replay("act", e)

            @block.vector
            def _(e):
                replay("dve", e)

            @block.gpsimd
            def _(e):
                replay("pool", e)

            @block.sync
            def _(e):
                replay("sp", e)

    def mm(self, out, lhsT, rhs, start, stop):
        oa, la, ra = out.ap, lhsT.ap, rhs.ap
        self.op("pe", lambda e: e.matmul(oa, la, ra, start=start, stop=stop),
                reads=[lhsT, rhs], writes=[out])

    def transpose(self, out, in_, ident):
        oa, ia, da = out.ap, in_.ap, ident.ap
        self.op("pe", lambda e: e.transpose(oa, ia, da), reads=[in_, ident], writes=[out])

    def act(self, out, in_, func, bias=None, scale=None, accum_out=None, eng="act"):
        oa, ia = out.ap, in_.ap
        reads = [in_]
        kw = {}
        if bias is not None:
            if isinstance(bias, V):
                reads.append(bias)
                kw["bias"] = bias.ap
            else:
                kw["bias"] = bias
        if scale is not None:
            if isinstance(scale, V):
                reads.append(scale)
                kw["scale"] = scale.ap
            else:
                kw["scale"] = scale
        writes = [out]
        if accum_out is not None:
            writes.append(accum_out)
            kw["accum_out"] = accum_out.ap
        self.op(eng, lambda e: e.activation(oa, ia, func, **kw), reads=reads, writes=writes)

    def tt(self, eng, out, in0, in1, op):
        oa, a, b = out.ap, in0.ap, in1.ap
        self.op(eng, lambda e: e.tensor_tensor(oa, a, b, op), reads=[in0, in1], writes=[out])

    def ts(self, eng, out, in0, s1, s2, op0, op1=None, accum_out=None):
        oa, a = out.ap, in0.ap
        reads = [in0]
        if isinstance(s1, V):
            reads.append(s1)
            s1 = s1.ap
        if isinstance(s2, V):
            reads.append(s2)
            s2 = s2.ap
        kw = {}
        if op1 is not None:
            kw["op1"] = op1
        writes = [out]
        if accum_out is not None:
            writes.append(accum_out)
            kw["accum_out"] = accum_out.ap
        self.op(eng, lambda e: e.tensor_scalar(oa, a, s1, s2, op0, **kw), reads=reads, writes=writes)

    def stt(self, out, in0, scalar, in1, op0, op1, eng="dve"):
        oa, a, b = out.ap, in0.ap, in1.ap
        reads = [in0, in1]
        if isinstance(scalar, V):
            reads.append(scalar)
            scalar = scalar.ap
        self.op(eng, lambda e: e.scalar_tensor_tensor(oa, a, scalar, b, op0, op1),
                reads=reads, writes=[out])

    def copy(self, eng, out, in_):
        oa, ia = out.ap, in_.ap
        if eng == "act":
            self.op(eng, lambda e: e.copy(oa, ia), reads=[in_], writes=[out])
        else:
            self.op(eng, lambda e: e.tensor_copy(oa, ia), reads=[in_], writes=[out])

    def memset(self, eng, out, val):
        oa = out.ap
        self.op(eng, lambda e: e.memset(oa, val), reads=[], writes=[out])

    def recip(self, out, in_):
        oa, ia = out.ap, in_.ap
        self.op("dve", lambda e: e.reciprocal(oa, ia), reads=[in_], writes=[out])


class Dram:
    def __init__(self, ap, name):
        self.ap = ap
        self.name = name
        self.bufs = {}

    def v(self, ap, k=0):
        b = self.bufs.get(k)
        if b is None:
            b = self.bufs[k] = Buf(f"{self.name}.{k}")
        return V(ap, b)

D = 1024
KD = 8
EPS = 1e-6
TB = 512
D_FF = 2816
JF = 22


class Ctx:
    def __init__(self, S, gs, consts_d):
        self.S = S
        self.banks = [S.psum(gs, f"bank{i}", [128, 512], F32) for i in range(8)]
        self.bi = 0
        self.ones_b = S.sbuf(gs, "ones_b", [128, 128], BF16)
        self.ident_f = S.sbuf(gs, "ident_f", [128, 128], F32)
        self.ident_b = S.sbuf(gs, "ident_b", [128, 128], BF16)
        S.memset("pool", self.ones_b[:], 1.0)
        S.dma("sp", self.ident_f[:], consts_d.v(consts_d.ap[0:128, 0:128]))
        S.dma("pool", self.ident_b[:], consts_d.v(consts_d.ap[0:128, 0:128]))

    def bank(self):
        b = self.banks[self.bi]
        self.bi = (self.bi + 1) % 8
        return b


def load_colvec(S, q, tile, dram, n):
    S.dma(q, tile[:], dram.v(dram.ap.rearrange("(k p) -> p k", p=128)), allow_slow_non_contiguous=True)


def rms_block(C, xb, lnw, hn, sq2, rstd, tmp, nt=TB, kd=KD, dim=D):
    S = C.S
    bk = C.bank()
    for k in range(kd):
        s = sq2[:, k % 2, 0:nt]
        S.act(s, xb[:, k, 0:nt], AF.Square)
        S.mm(bk[:, 0:nt], C.ones_b[:], s, start=(k == 0), stop=(k == kd - 1))
    S.ts("dve", tmp[:, 0:nt], bk[:, 0:nt], 1.0 / dim, EPS, ALU.mult, ALU.add)
    S.act(tmp[:, 0:nt], tmp[:, 0:nt], AF.Sqrt)
    S.recip(rstd[:, 0:nt], tmp[:, 0:nt])
    for k in range(kd):
        S.stt(hn[:, k, 0:nt], xb[:, k, 0:nt], lnw[:, k:k + 1], rstd[:, 0:nt], ALU.mult, ALU.mult)


def ffn_layer(C, x_src, x_dst, ln_d, up_d, cw_d, cb_d, down_d, NT, SEQ):
    S = C.S
    TB = 512
    with ExitStack() as ls:
        wup = S.sbuf(ls, "wup", [128, KD, 2 * D_FF], BF16)
        wdn = S.sbuf(ls, "wdn", [128, JF, D], BF16)
        lnw = S.sbuf(ls, "f_lnw", [128, KD], F32)
        cw = S.sbuf(ls, "f_cw", [128, 3, JF], F32)
        cb = S.sbuf(ls, "f_cb", [128, JF], F32)
        halo = S.sbuf(ls, "f_halo", [128, JF, 2], F32)
        xbs = [S.sbuf(ls, f"f_xb{i}", [128, KD, TB], F32) for i in range(1)]
        sq2 = S.sbuf(ls, "f_sq", [128, 2, TB], BF16)
        rstd = S.sbuf(ls, "f_rstd", [128, TB], F32)
        tmp = S.sbuf(ls, "f_tmp", [128, TB], F32)
        hn = S.sbuf(ls, "f_hn", [128, KD, TB], BF16)
        gext = [S.sbuf(ls, f"f_gext{i}", [128, TB + 2], F32) for i in range(2)]
        cv = [S.sbuf(ls, f"f_cv{i}", [128, TB], F32) for i in range(2)]
        sg = [S.sbuf(ls, f"f_sg{i}", [128, TB], F32) for i in range(2)]
        h = S.sbuf(ls, "f_h", [128, JF, TB], BF16)

        load_colvec(S, "sp", lnw, ln_d, KD)
        load_colvec(S, "sp", cb, cb_d, JF)
        S.dma("sp", cw[:], cw_d.v(cw_d.ap.rearrange("w (j p) -> p w j", p=128)), allow_slow_non_contiguous=True)
        upv = up_d.ap.rearrange("(k p) n -> p k n", p=128)
        for k in range(KD):
            S.dma("pool", wup.p(k)[:, k, :], up_d.v(upv[:, k, :]))
        dnv = down_d.ap.rearrange("(j p) n -> p j n", p=128)
        for j0 in range(0, JF, 6):
            j1 = min(JF, j0 + 6)
            S.dma("pool", wdn.p(j0)[:, j0:j1, :], down_d.v(dnv[:, j0:j1, :]))

        nblk = NT // TB
        xsv = x_src.ap.rearrange("(k p) t -> p k t", p=128)
        xdv = x_dst.ap.rearrange("(k p) t -> p k t", p=128)

        def load(i):
            S.dma("sp", xbs[0][:], x_src.v(xsv[:, :, i * TB:(i + 1) * TB], i))

        for i in range(nblk):
            t0 = i * TB
            xb = xbs[0]
            load(i)
            if t0 % SEQ == 0:
                S.memset("pool", halo[:], 0.0)
            rms_block(C, xb, lnw, hn, sq2, rstd, tmp, nt=TB)
            for j in range(JF):
                pg = C.bank()
                pu = C.bank()
                for k in range(KD):
                    S.mm(pg[:, 0:TB], wup.p(k)[:, k, j * 128:(j + 1) * 128], hn[:, k, :], start=(k == 0), stop=(k == KD - 1))
                for k in range(KD):
                    S.mm(pu[:, 0:TB], wup.p(k)[:, k, D_FF + j * 128:D_FF + (j + 1) * 128], hn[:, k, :], start=(k == 0), stop=(k == KD - 1))
                g = gext[j % 2]
                S.copy("pool", g[:, 0:2], halo[:, j, :])
                S.copy("act", g[:, 2:TB + 2], pg[:, 0:TB])
                S.copy("pool", halo[:, j, :], g[:, TB:TB + 2])
                c = cv[j % 2]
                S.ts("dve", c[:], g[:, 0:TB], cw[:, 0, j:j + 1], None, ALU.mult)
                S.stt(c[:], g[:, 1:TB + 1], cw[:, 1, j:j + 1], c[:], ALU.mult, ALU.add)
                S.stt(c[:], g[:, 2:TB + 2], cw[:, 2, j:j + 1], c[:], ALU.mult, ALU.add)
                s_ = sg[j % 2]
                S.act(s_[:], c[:], AF.Silu, bias=cb[:, j:j + 1])
                S.tt("dve", h.p(j)[:, j, :], s_[:], pu[:, 0:TB], ALU.mult)
            j0s = list(range(0, JF, 6))
            for m in range(KD):
                po = C.bank()
                for j in range(JF):
                    S.mm(po[:, 0:TB], wdn.p(j0s[j // 6])[:, j, m * 128:(m + 1) * 128], h.p(j)[:, j, :], start=(j == 0), stop=(j == JF - 1))
                S.tt("dve", xb[:, m, :], xb[:, m, :], po[:, 0:TB], ALU.add)
            S.dma("sp", x_dst.v(xdv[:, :, t0:t0 + TB], i), xb[:])
    S.barrier()


def xattn_layer(C, x_src, x_dst, mem_d, lnx_d, lnm_d, wq_d, wkv_d, wo_d, NT, SEQ, NMEM=256):
    S = C.S
    TB = 512
    with ExitStack() as ls:
        wq = S.sbuf(ls, "x_wq", [128, KD, D], BF16)
        wkv = S.sbuf(ls, "x_wkv", [128, KD, 2 * D], BF16)
        wo = S.sbuf(ls, "x_wo", [128, KD, D], BF16)
        lnx = S.sbuf(ls, "x_lnx", [128, KD], F32)
        lnm = S.sbuf(ls, "x_lnm", [128, KD], F32)
        memb = S.sbuf(ls, "x_memb", [128, KD, NMEM], F32)
        memn = S.sbuf(ls, "x_memn", [128, KD, NMEM], BF16)
        kT = S.sbuf(ls, "x_kT", [128, KD, NMEM], BF16)
        vt = S.sbuf(ls, "x_v", [128, 2, D], BF16)
        xbs = [S.sbuf(ls, f"x_xb{i}", [128, KD, TB], F32) for i in range(2)]
        sq2 = S.sbuf(ls, "x_sq", [128, 2, TB], BF16)
        rstd = S.sbuf(ls, "x_rstd", [128, TB], F32)
        tmp = S.sbuf(ls, "x_tmp", [128, TB], F32)
        hn = S.sbuf(ls, "x_hn", [128, KD, TB], BF16)
        qT = S.sbuf(ls, "x_qT", [128, KD, TB], BF16)
        pT = [S.sbuf(ls, f"x_pT{i}", [128, 2, TB], BF16) for i in range(2)]
        rs = [S.sbuf(ls, f"x_rs{i}", [128, TB], F32) for i in range(2)]
        oT = S.sbuf(ls, "x_oT", [128, KD, TB], BF16)

        load_colvec(S, "sp", lnx, lnx_d, KD)
        load_colvec(S, "sp", lnm, lnm_d, KD)
        S.dma("pool", wq[:], wq_d.v(wq_d.ap.rearrange("(k p) n -> p k n", p=128)))
        kvv = wkv_d.ap.rearrange("(k p) n -> p k n", p=128)
        S.dma("pool", wkv.p(0)[:, :, 0:D], wkv_d.v(kvv[:, :, 0:D]))
        S.dma("pool", wkv.p(1)[:, :, D:2 * D], wkv_d.v(kvv[:, :, D:2 * D]))
        S.dma("pool", wo[:], wo_d.v(wo_d.ap.rearrange("(k p) n -> p k n", p=128)))

        nblk = NT // TB
        bps = SEQ // TB
        xsv = x_src.ap.rearrange("(k p) t -> p k t", p=128)
        xdv = x_dst.ap.rearrange("(k p) t -> p k t", p=128)
        mv = mem_d.ap.rearrange("(k p) t -> p k t", p=128)

        def load(i):
            S.dma("sp", xbs[i % 2][:], x_src.v(xsv[:, :, i * TB:(i + 1) * TB], i))

        load(0)
        for i in range(nblk):
            t0 = i * TB
            xb = xbs[i % 2]
            if i + 1 < nblk:
                load(i + 1)
            if i % bps == 0:
                b = i // bps
                S.dma("sp", memb[:], mem_d.v(mv[:, :, b * NMEM:(b + 1) * NMEM], b))
                rms_block(C, memb, lnm, memn, sq2, rstd, tmp, nt=NMEM)
                for c in range(KD):
                    bk = C.bank()
                    for k in range(KD):
                        S.mm(bk[:, 0:NMEM], wkv.p(0)[:, k, c * 128:(c + 1) * 128], memn[:, k, :], start=(k == 0), stop=(k == KD - 1))
                    S.copy("act", kT[:, c, :], bk[:, 0:NMEM])
                for mt in range(2):
                    for hf in range(2):
                        bk = C.bank()
                        for k in range(KD):
                            S.mm(bk[:], memn[:, k, mt * 128:(mt + 1) * 128], wkv.p(1)[:, k, D + hf * 512:D + (hf + 1) * 512], start=(k == 0), stop=(k == KD - 1))
                        S.copy("act", vt[:, mt, hf * 512:(hf + 1) * 512], bk[:])
            rms_block(C, xb, lnx, hn, sq2, rstd, tmp, nt=TB)
            for c in range(KD):
                bk = C.bank()
                for k in range(KD):
                    S.mm(bk[:], wq[:, k, c * 128:(c + 1) * 128], hn[:, k, :], start=(k == 0), stop=(k == KD - 1))
                S.act(qT.p(c)[:, c, :], bk[:], AF.Copy, scale=1.0 / 16.0)
            for hh in range(4):
                p_ = pT[hh % 2]
                r_ = rs[hh % 2]
                for mt in range(2):
                    bk = C.bank()
                    for dc in range(2):
                        c = 2 * hh + dc
                        S.mm(bk[:], kT[:, c, mt * 128:(mt + 1) * 128], qT.p(c)[:, c, :], start=(dc == 0), stop=(dc == 1))
                    S.act(p_[:, mt, :], bk[:], AF.Exp)
                bs = C.bank()
                for mt in range(2):
                    S.mm(bs[:], C.ones_b[:], p_[:, mt, :], start=(mt == 0), stop=(mt == 1))
                S.recip(r_[:], bs[:])
                for dc in range(2):
                    c = 2 * hh + dc
                    bk = C.bank()
                    for mt in range(2):
                        S.mm(bk[:], vt[:, mt, c * 128:(c + 1) * 128], p_[:, mt, :], start=(mt == 0), stop=(mt == 1))
                    S.tt("dve", oT.p(c)[:, c, :], bk[:], r_[:], ALU.mult)
            for m in range(KD):
                po = C.bank()
                for c in range(KD):
                    S.mm(po[:], wo[:, c, m * 128:(m + 1) * 128], oT.p(c)[:, c, :], start=(c == 0), stop=(c == KD - 1))
                S.tt("dve", xb[:, m, :], xb[:, m, :], po[:], ALU.add)
            S.dma("sp", x_dst.v(xdv[:, :, t0:t0 + TB], i), xb[:])
    S.barrier()


class Rot:
    def __init__(self, items):
        self.items = items
        self.i = 0

    def __call__(self):
        b = self.items[self.i]
        self.i = (self.i + 1) % len(self.items)
        return b


def make_consts():
    c = np.zeros((128, 512), np.float32)
    c[:, 0:128] = np.eye(128)
    s = np.arange(128)[:, None]
    l = np.arange(128)[None, :]
    c[:, 128:256] = (s <= l)
    c[:, 256:384] = np.where(s <= l, 0.0, -1e6)
    c[:, 384:512] = 1.0
    return c


def load_consts(C, gs, consts_d):
    S = C.S
    C.cf = S.sbuf(gs, "cf", [128, 512], F32)
    S.dma("sp", C.cf[:], consts_d.v(consts_d.ap[:, 0:512]))
    C.triu = C.cf[:, 128:256]
    C.negmask = C.cf[:, 256:384]
    C.ones_f = C.cf[:, 384:512]
    C.idf = C.cf[:, 0:128]


def softplus_small(S, out, xin, t1, t2):
    S.stt(t1, xin, -1.0, xin, ALU.mult, ALU.max)
    S.act(t2, t1, AF.Exp, scale=-1.0)
    S.ts("dve", t2, t2, 1.0, None, ALU.add)
    S.act(t2, t2, AF.Ln)
    S.stt(out, xin, 0.0, t2, ALU.max, ALU.add)


def ssd_layer(C, x_src, x_dst, ln_d, inw_d, cw_d, cb_d, dtb_d, alog_d, dsk_d, nw_d, ow_d, NT, SEQ):
    S = C.S
    TQ = 128
    DI = 2048
    NH = 32
    with ExitStack() as ls:
        win = S.sbuf(ls, "m_win", [128, KD, 6176], BF16)
        wout = S.sbuf(ls, "m_wout", [128, 16, D], BF16)
        lnw = S.sbuf(ls, "m_lnw", [128, KD], F32)
        cw = S.sbuf(ls, "m_cw", [128, 4, 32], F32)
        cb = S.sbuf(ls, "m_cb", [128, 32], F32)
        nw = S.sbuf(ls, "m_nw", [128, 16], F32)
        dtb_r = S.sbuf(ls, "m_dtb", [128, NH], F32)
        a_r = S.sbuf(ls, "m_a", [128, NH], F32)
        dsk_r = S.sbuf(ls, "m_dsk", [128, NH], F32)
        halo = S.sbuf(ls, "m_halo", [128, 32, 3], F32)
        xbs = [S.sbuf(ls, f"m_xb{i}", [128, KD, TQ], F32) for i in range(1)]
        sq2 = S.sbuf(ls, "m_sq", [128, 2, TQ], BF16)
        rstd = S.sbuf(ls, "m_rstd", [128, 512], F32)
        tmp = S.sbuf(ls, "m_tmp", [128, 512], F32)
        hn = S.sbuf(ls, "m_hn", [128, KD, TQ], BF16)
        zs = S.sbuf(ls, "m_zs", [128, 16, TQ], BF16)
        xc = S.sbuf(ls, "m_xc", [128, 8, TQ], F32)
        Bc = S.sbuf(ls, "m_Bc", [128, 8, TQ], F32)
        Bcb = S.sbuf(ls, "m_Bcb", [128, 8, TQ], BF16)
        Ccb = S.sbuf(ls, "m_Ccb", [128, 8, TQ], BF16)
        gext = [S.sbuf(ls, f"m_gext{i}", [128, TQ + 3], F32) for i in range(2)]
        cv = [S.sbuf(ls, f"m_cv{i}", [128, TQ], F32) for i in range(2)]
        sm = S.sbuf(ls, "m_sm", [128, 10, NH], F32)
        acumT = S.sbuf(ls, "m_acumT", [32, TQ], F32)
        nacumT = S.sbuf(ls, "m_nacumT", [32, TQ], F32)
        xdt = S.sbuf(ls, "m_xdt", [128, NH, 64], BF16)
        xdtd = S.sbuf(ls, "m_xdtd", [128, NH, 64], BF16)
        xD = S.sbuf(ls, "m_xD", [128, NH, 64], BF16)
        B_tm = S.sbuf(ls, "m_Btm", [128, 8, 128], BF16)
        state = S.sbuf(ls, "m_state", [128, NH, 64], F32)
        state_b = S.sbuf(ls, "m_stateb", [128, NH, 64], BF16)
        dm = [S.sbuf(ls, f"m_dm{i}", [128, TQ], F32) for i in range(3)]
        MT = [S.sbuf(ls, f"m_MT{i}", [128, TQ], BF16) for i in range(3)]
        Ebc = [S.sbuf(ls, f"m_Ebc{i}", [128, TQ], F32) for i in range(3)]
        Cdec = [S.sbuf(ls, f"m_Cdec{i}", [128, TQ], BF16) for i in range(3)]
        yz = S.sbuf(ls, "m_yz", [128, 16, TQ], F32)
        yn = S.sbuf(ls, "m_yn", [128, 16, TQ], BF16)

        dtr, t1, t2, dtt, dA, acum, elast, dl, dtdl, t3 = [sm[:, i, :] for i in range(10)]

        load_colvec(S, "sp", lnw, ln_d, KD)
        load_colvec(S, "sp", cb, cb_d, 32)
        load_colvec(S, "sp", nw, nw_d, 16)
        S.dma("sp", cw[:], cw_d.v(cw_d.ap.rearrange("w (j p) -> p w j", p=128)), allow_slow_non_contiguous=True)
        S.dma("sp", dtb_r[:], dtb_d.v(dtb_d.ap.partition_broadcast(128)))
        S.dma("sp", a_r[:], alog_d.v(alog_d.ap.partition_broadcast(128)))
        S.dma("sp", dsk_r[:], dsk_d.v(dsk_d.ap.partition_broadcast(128)))
        S.act(a_r[:], a_r[:], AF.Exp)
        S.ts("dve", a_r[:], a_r[:], -1.0, None, ALU.mult)
        inv = inw_d.ap.rearrange("(k p) n -> p k n", p=128)
        for k in range(KD):
            S.dma("pool", win.p(k)[:, k, :], inw_d.v(inv[:, k, :]))
        owv = ow_d.ap.rearrange("(c p) n -> p c n", p=128)
        for c0 in range(0, 16, 8):
            S.dma("pool", wout.p(c0)[:, c0:c0 + 8, :], ow_d.v(owv[:, c0:c0 + 8, :]))

        ntile = NT // TQ
        xsv = x_src.ap.rearrange("(k p) t -> p k t", p=128)
        xdv = x_dst.ap.rearrange("(k p) t -> p k t", p=128)
        rot = Rot(C.banks[0:4])
        cbt = C.banks[4:6]
        hbanks = C.banks[6:8]

        def load(i):
            S.dma("sp", xbs[0][:], x_src.v(xsv[:, :, i * TQ:(i + 1) * TQ], i))

        def ind(h):
            return V(C.cf.t[0:32, h:h + 1].to_broadcast([32, 128]), C.cf._buf(0))

        load(0)
        for i in range(ntile):
            t0 = i * TQ
            xb = xbs[0]
            if i > 0:
                load(i)
            if t0 % SEQ == 0:
                S.memset("pool", halo[:], 0.0)
                S.memset("pool", state[:], 0.0)
                S.memset("pool", state_b[:], 0.0)
            rms_block(C, xb, lnw, hn, sq2, rstd, tmp, nt=TQ)
            bk = rot()
            for k in range(KD):
                S.mm(bk[:, 0:NH], hn[:, k, :], win.p(k)[:, k, 6144:6176], start=(k == 0), stop=(k == KD - 1))
            S.tt("dve", dtr, bk[:, 0:NH], dtb_r[:], ALU.add)
            softplus_small(S, dtt, dtr, t1, t2)
            S.tt("dve", dA, dtt, a_r[:], ALU.mult)
            bk = rot()
            S.mm(bk[:, 0:NH], C.triu, dA, start=True, stop=True)
            S.mm(bk[0:32, 128:256], dA, C.triu, start=True, stop=True)
            S.mm(bk[:, 256:256 + NH], C.ones_f, dA, start=True, stop=True)
            S.copy("act", acum, bk[:, 0:NH])
            S.copy("act", acumT[:], bk[0:32, 128:256])
            S.act(nacumT[:], bk[0:32, 128:256], AF.Copy, scale=-1.0)
            S.act(elast, bk[:, 256:256 + NH], AF.Exp)
            S.tt("dve", t3, bk[:, 256:256 + NH], acum, ALU.subtract)
            S.act(dl, t3, AF.Exp)
            S.tt("dve", dtdl, dl, dtt, ALU.mult)
            for q4 in range(12):
                bk = rot()
                for u in range(4):
                    ct = q4 * 4 + u
                    for k in range(KD):
                        S.mm(bk[:, u * 128:(u + 1) * 128], win.p(k)[:, k, ct * 128:(ct + 1) * 128], hn[:, k, :],
                             start=(k == 0), stop=(k == KD - 1))
                if q4 < 4:
                    S.act(zs[:, q4 * 4:(q4 + 1) * 4, :], V(bk.t[:, :].rearrange("p (a b) -> p a b", a=4), bk._buf(0)), AF.Silu)
                    continue
                for u in range(4):
                    j = (q4 - 4) * 4 + u
                    g = gext[j % 2]
                    S.copy("pool", g[:, 0:3], halo[:, j, :])
                    S.copy("act", g[:, 3:TQ + 3], bk[:, u * 128:(u + 1) * 128])
                    S.copy("pool", halo[:, j, :], g[:, TQ:TQ + 3])
                    c = cv[j % 2]
                    S.ts("dve", c[:], g[:, 0:TQ], cw[:, 0, j:j + 1], None, ALU.mult)
                    for w in range(1, 4):
                        S.stt(c[:], g[:, w:TQ + w], cw[:, w, j:j + 1], c[:], ALU.mult, ALU.add)
                    if j < 16:
                        S.act(xc.p(j % 8)[:, j % 8, :], c[:], AF.Silu, bias=cb[:, j:j + 1])
                        if j % 4 == 3:
                            tb_ = rot()
                            for u2 in range(4):
                                j2 = j - 3 + u2
                                S.transpose(tb_[:, u2 * 128:(u2 + 1) * 128], xc.p(j2 % 8)[:, j2 % 8, :], C.idf)
                            hs_ = slice((j // 4) * 8, (j // 4 + 1) * 8)
                            tbv = V(tb_.t[:, :].rearrange("p (a b) -> p a b", a=8), tb_._buf(0))

                            def bc8(v):
                                return V(v.ap[:, hs_].unsqueeze(2).to_broadcast([128, 8, 64]), v.buf)

                            S.tt("dve", xdt[:, hs_, :], tbv, bc8(dtt), ALU.mult)
                            S.tt("dve", xdtd[:, hs_, :], tbv, bc8(dtdl), ALU.mult)
                            S.tt("dve", xD[:, hs_, :], tbv, bc8(dsk_r[:]), ALU.mult)
                    elif j < 24:
                        S.act(Bc.p(j)[:, j - 16, :], c[:], AF.Silu, bias=cb[:, j:j + 1])
                        S.copy("pool", Bcb.p(j)[:, j - 16, :], Bc.p(j)[:, j - 16, :])
                    else:
                        S.act(Ccb.p(j)[:, j - 24, :], c[:], AF.Silu, bias=cb[:, j:j + 1])
            for q4 in range(2):
                bk = rot()
                for u in range(4):
                    j = q4 * 4 + u
                    S.transpose(bk[:, u * 128:(u + 1) * 128], Bc.p(16 + j)[:, j, :], C.idf)
                S.copy("act", V(B_tm.t[:, q4 * 4:(q4 + 1) * 4, :].rearrange("p a b -> p (a b)"), B_tm._buf(0)), bk[:])

            for g in range(8):
                S.mm(cbt[g // 4][:, (g % 4) * 128:(g % 4 + 1) * 128], Bcb.p(16 + g)[:, g, :], Ccb.p(24 + g)[:, g, :], start=True, stop=True)
            ybanks = [rot() for _ in range(4)]

            def stage_a(h):
                g = h // 4
                hb = hbanks[(h // 2) % 2]
                o = (h % 2) * 256
                r = h % 3
                S.mm(hb[:, o:o + 128], ind(h), acumT[:], start=True, stop=False)
                S.mm(hb[:, o:o + 128], nacumT[:], ind(h), start=False, stop=True)
                S.mm(hb[:, o + 128:o + 256], ind(h), acumT[:], start=True, stop=True)
                S.tt("dve", dm[r][:], hb[:, o:o + 128], C.negmask, ALU.add)
                S.act(dm[r][:], dm[r][:], AF.Exp)
                S.tt("dve", MT[r][:], dm[r][:], cbt[g // 4][:, (g % 4) * 128:(g % 4 + 1) * 128], ALU.mult)
                S.act(Ebc[r][:], hb[:, o + 128:o + 256], AF.Exp)
                S.tt("pool", Cdec[r][:], Ccb.p(24 + g)[:, g, :], Ebc[r][:], ALU.mult)

            def stage_b(h):
                c = h // 2
                hh = h % 2
                r = h % 3
                ybank = ybanks[c // 4]
                ycol = (c % 4) * 128
                yo = ybank.p(c)[hh * 64:(hh + 1) * 64, ycol:ycol + 128]
                S.mm(yo, xdt[:, h, :], MT[r][:], start=True, stop=False)
                S.mm(yo, state_b[:, h, :], Cdec[r][:], start=False, stop=False)
                S.mm(yo, xD[:, h, :], C.ident_b[:], start=False, stop=True)
                if hh == 1:
                    S.tt("dve", yz.p(c)[:, c, :], ybank.p(c)[:, ycol:ycol + 128], zs[:, c, :], ALU.mult)

            stage_a(0)
            for h in range(NH):
                if h + 1 < NH:
                    stage_a(h + 1)
                stage_b(h)
            for gp in range(4):
                bk = rot()
                for u in range(2):
                    g = gp * 2 + u
                    S.mm(bk[:, u * 256:(u + 1) * 256], B_tm[:, g, :],
                         V(xdtd.t[:, 4 * g:4 * g + 4, :].rearrange("p a b -> p (a b)"), xdtd._buf(0)), start=True, stop=True)
                hs = slice(gp * 8, gp * 8 + 8)
                S.tt("pool", state[:, hs, :], state[:, hs, :], V(elast.ap[:, hs].unsqueeze(2).to_broadcast([128, 8, 64]), elast.buf), ALU.mult)
                S.tt("dve", state[:, hs, :], state[:, hs, :], V(bk.t[:, :].rearrange("p (a b) -> p a b", a=8), bk._buf(0)), ALU.add)
            S.copy("act", state_b[:], state[:])
            gb = [rot(), rot()]
            for c in range(16):
                g = c // 2
                s = sq2[:, c % 2, 0:TQ]
                S.act(s, yz.p(c)[:, c, :], AF.Square)
                S.mm(gb[g // 4][:, (g % 4) * 128:(g % 4 + 1) * 128], C.ones_b[:], s, start=(c % 2 == 0), stop=(c % 2 == 1))
            rs2 = [rstd, tmp]
            for q in range(2):
                S.ts("dve", rs2[q][:], gb[q][:], 1.0 / 256.0, EPS, ALU.mult, ALU.add)
                S.act(rs2[q][:], rs2[q][:], AF.Sqrt)
                S.recip(rs2[q][:], rs2[q][:])
            for c in range(16):
                g = c // 2
                S.stt(yn.p(c)[:, c, :], yz.p(c)[:, c, :], nw[:, c:c + 1], rs2[g // 4][:, (g % 4) * 128:(g % 4 + 1) * 128], ALU.mult, ALU.mult)
            for m4 in range(2):
                bk = rot()
                for u in range(4):
                    m = m4 * 4 + u
                    for c in range(16):
                        S.mm(bk[:, u * 128:(u + 1) * 128], wout.p((c // 8) * 8)[:, c, m * 128:(m + 1) * 128], yn.p(c)[:, c, :],
                             start=(c == 0), stop=(c == 15))
                for u in range(4):
                    m = m4 * 4 + u
                    S.tt("dve", xb[:, m, :], xb[:, m, :], bk[:, u * 128:(u + 1) * 128], ALU.add)
            S.dma("sp", x_dst.v(xdv[:, :, t0:t0 + TQ], i), xb[:])
    S.barrier()


def make_consts2():
    c = np.zeros((128, 260), np.float32)
    t = np.arange(128)
    for cc in range(4):
        c[:, 256 + cc] = (t // 32 == cc)
    c[:, 0:128] = (t % 32 != 0)[None, :].astype(np.float32)
    s = t[:, None]
    l = t[None, :]
    c[:, 128:256] = ((s // 32 == l // 32) & (s <= l)).astype(np.float32)
    return c


def hgrn_layer(C, x_src, x_dst, ln_d, inw_d, hlb_d, nw_d, ow_d, consts2_d, layer_idx, depth, NT, SEQ):
    S = C.S
    TQ = 128
    NHD = 8
    QS = 128 ** -0.5
    with ExitStack() as ls:
        win = S.sbuf(ls, "h_win", [128, KD, 4 * D], BF16)
        wout = S.sbuf(ls, "h_wout", [128, KD, D], BF16)
        lnw = S.sbuf(ls, "h_lnw", [128, KD], F32)
        nw = S.sbuf(ls, "h_nw", [128, 1], F32)
        hlb = S.sbuf(ls, "h_hlb", [128, depth, KD], F32)
        lbs = S.sbuf(ls, "h_lbs", [128, 4, KD], F32)
        c2 = S.sbuf(ls, "h_c2", [128, 260], F32)
        smask = S.sbuf(ls, "h_smask", [128, NHD * TQ], F32)
        xbs = [S.sbuf(ls, f"h_xb{i}", [128, KD, TQ], F32) for i in range(2)]
        sq2 = S.sbuf(ls, "h_sq", [128, 2, TQ], BF16)
        rstd = S.sbuf(ls, "h_rstd", [128, 512], F32)
        tmp = S.sbuf(ls, "h_tmp", [128, 512], F32)
        hn = S.sbuf(ls, "h_hn", [128, KD, TQ], BF16)
        qs = S.sbuf(ls, "h_qs", [128, NHD * TQ], F32)
        sg = S.sbuf(ls, "h_sg", [128, NHD * TQ], F32)
        fg = S.sbuf(ls, "h_fg", [128, NHD * TQ], F32)
        lf = S.sbuf(ls, "h_lf", [128, NHD * TQ], F32)
        kk = S.sbuf(ls, "h_kk", [128, NHD * TQ], F32)
        gc = S.sbuf(ls, "h_gc", [128, NHD * TQ], F32)
        egc = S.sbuf(ls, "h_egc", [128, NHD * TQ], F32)
        engc = S.sbuf(ls, "h_engc", [128, NHD * TQ], F32)
        ek = S.sbuf(ls, "h_ek", [128, NHD * TQ], F32)
        kend = S.sbuf(ls, "h_kend", [128, NHD * TQ], F32)
        qdec = S.sbuf(ls, "h_qdec", [128, NHD * TQ], BF16)
        kinv = S.sbuf(ls, "h_kinv", [128, NHD * TQ], BF16)
        v_tm = S.sbuf(ls, "h_vtm", [128, NHD * 128], BF16)
        kend_tm = [S.sbuf(ls, f"h_kendtm{i}", [128, 4, 128], BF16) for i in range(2)]
        attm = [S.sbuf(ls, f"h_attm{i}", [128, 128], BF16) for i in range(2)]
        state = S.sbuf(ls, "h_state", [128, NHD, 128], F32)
        state_b0 = S.sbuf(ls, "h_stb0", [128, NHD, 128], BF16)
        stb = [S.sbuf(ls, f"h_stbr{i}", [128, 3, 128], BF16) for i in range(2)]
        o_f = S.sbuf(ls, "h_of", [128, NHD, TQ], F32)
        yt = S.sbuf(ls, "h_yt", [128, TQ], F32)
        yn = S.sbuf(ls, "h_yn", [128, NHD, TQ], BF16)

        load_colvec(S, "sp", lnw, ln_d, KD)
        S.dma("sp", nw[:], nw_d.v(nw_d.ap.rearrange("(p o) -> p o", o=1)))
        S.dma("sp", hlb[:], hlb_d.v(hlb_d.ap.rearrange("j (k p) -> p j k", p=128)), allow_slow_non_contiguous=True)
        S.dma("sp", c2[:], consts2_d.v(consts2_d.ap[:, :]))
        for hd in range(NHD):
            S.dma("sp", smask[:, hd * TQ:(hd + 1) * TQ], consts2_d.v(consts2_d.ap[:, 0:128]))
        bdmask = c2[:, 128:256]
        S.act(hlb[:], hlb[:], AF.Exp)
        S.copy("dve", lbs[:, 0, :], hlb[:, 0, :])
        for j in range(1, depth):
            S.tt("dve", lbs[:, 0, :], lbs[:, 0, :], hlb[:, j, :], ALU.add)
        S.recip(lbs[:, 0, :], lbs[:, 0, :])
        S.memset("dve", lbs[:, 3, :], 0.0)
        for j in range(1, layer_idx + 1):
            S.tt("dve", lbs[:, 3, :], lbs[:, 3, :], hlb[:, j, :], ALU.add)
        S.tt("dve", lbs[:, 1, :], lbs[:, 3, :], lbs[:, 0, :], ALU.mult)
        S.ts("dve", lbs[:, 2, :], lbs[:, 1, :], -1.0, 1.0, ALU.mult, ALU.add)
        inv = inw_d.ap.rearrange("(k p) n -> p k n", p=128)
        for k in range(KD):
            S.dma("pool", win.p(k)[:, k, :], inw_d.v(inv[:, k, :]))
        S.dma("pool", wout[:], ow_d.v(ow_d.ap.rearrange("(c p) n -> p c n", p=128)))

        ntile = NT // TQ
        xsv = x_src.ap.rearrange("(k p) t -> p k t", p=128)
        xdv = x_dst.ap.rearrange("(k p) t -> p k t", p=128)
        rot = Rot(C.banks[0:6])
        orot = Rot(C.banks[6:8])

        def load(i):
            S.dma("sp", xbs[i % 2][:], x_src.v(xsv[:, :, i * TQ:(i + 1) * TQ], i))

        load(0)
        for i in range(ntile):
            t0 = i * TQ
            xb = xbs[i % 2]
            if i + 1 < ntile:
                load(i + 1)
            if t0 % SEQ == 0:
                S.memset("pool", state[:], 0.0)
                S.memset("pool", state_b0[:], 0.0)
            rms_block(C, xb, lnw, hn, sq2, rstd, tmp, nt=TQ)
            for part, dst, fn in ((0, qs, AF.Silu), (1, fg, AF.Sigmoid), (3, sg, AF.Silu)):
                for q4 in range(2):
                    bk = rot()
                    for u in range(4):
                        ct = part * 8 + q4 * 4 + u
                        for k in range(KD):
                            S.mm(bk[:, u * 128:(u + 1) * 128], win.p(k)[:, k, ct * 128:(ct + 1) * 128], hn[:, k, :],
                                 start=(k == 0), stop=(k == KD - 1))
                    S.act(dst[:, q4 * 512:(q4 + 1) * 512], bk[:], fn)
            for hf in range(2):
                bk = rot()
                for k in range(KD):
                    S.mm(bk[:], hn[:, k, :], win.p(k)[:, k, 2 * D + hf * 512:2 * D + (hf + 1) * 512], start=(k == 0), stop=(k == KD - 1))
                S.copy("act", v_tm[:, hf * 512:(hf + 1) * 512], bk[:])
            for hd in range(NHD):
                sl = slice(hd * TQ, (hd + 1) * TQ)
                S.ts("dve", fg[:, sl], fg[:, sl], lbs[:, 2, hd:hd + 1], lbs[:, 1, hd:hd + 1], ALU.mult, ALU.add)
            S.act(lf[:], fg[:], AF.Ln)
            S.ts("pool", kk[:], fg[:], -1.0, 1.0, ALU.mult, ALU.add)
            lfa, sma, gca = lf[:].ap, smask[:].ap, gc[:].ap
            S.op("dve", lambda e, lfa=lfa, sma=sma, gca=gca: e.tensor_tensor_scan(gca, sma, lfa, 0.0, ALU.mult, ALU.add),
                 reads=[lf[:], smask[:]], writes=[gc[:]])
            S.act(egc[:], gc[:], AF.Exp)
            S.act(engc[:], gc[:], AF.Exp, scale=-1.0)
            gc3 = V(gc.t[:, :].rearrange("p (c q) -> p c q", q=32), gc._buf(0))
            gl3 = V(gc.t[:, :].rearrange("p (c q) -> p c q", q=32)[:, :, 31:32].to_broadcast([128, NHD * 4, 32]), gc._buf(0))
            ek3 = V(ek.t[:, :].rearrange("p (c q) -> p c q", q=32), ek._buf(0))
            S.tt("dve", ek3, gl3, gc3, ALU.subtract)
            S.act(ek[:], ek[:], AF.Exp)
            S.stt(qdec[:], qs[:], QS, egc[:], ALU.mult, ALU.mult)
            S.tt("pool", kinv[:], kk[:], engc[:], ALU.mult)
            S.tt("pool", kend[:], kk[:], ek[:], ALU.mult)
            for hd in range(NHD):
                sl = slice(hd * TQ, (hd + 1) * TQ)
                r = hd % 2
                bk = rot()
                S.transpose(bk[:, 0:128], kend[:, sl], C.idf)
                for c in range(4):
                    S.act(kend_tm[r][:, c, :], bk[:, 0:128], AF.Copy, scale=c2[:, 256 + c:257 + c])
                S.mm(bk[:, 128:256], kinv[:, sl], qdec[:, sl], start=True, stop=True)
                S.tt("dve", attm[r][:], bk[:, 128:256], bdmask, ALU.mult)
                dsb = rot()
                for c in range(4):
                    S.mm(dsb[:, c * 128:(c + 1) * 128], kend_tm[r][:, c, :], v_tm[:, hd * 128:(hd + 1) * 128], start=True, stop=True)
                sts = [state_b0[:, hd, :]] + [stb[r][:, c, :] for c in range(3)]
                for c in range(4):
                    col = hd * TQ + 32 * c + 31
                    S.stt(state[:, hd, :], state[:, hd, :], egc[:, col:col + 1], dsb[:, c * 128:(c + 1) * 128], ALU.mult, ALU.add)
                    if c < 3:
                        S.copy("act", stb[r][:, c, :], state[:, hd, :])
                ob = orot()
                oo = ob[:, 0:128]
                S.mm(oo, v_tm[:, hd * 128:(hd + 1) * 128], attm[r][:], start=True, stop=False)
                for c in range(4):
                    S.mm(ob[:, c * 32:(c + 1) * 32], sts[c], qdec[:, hd * TQ + 32 * c:hd * TQ + 32 * c + 32], start=False, stop=(c == 3))
                S.copy("act", state_b0[:, hd, :], state[:, hd, :])
                S.copy("act", o_f[:, hd, :], oo)
            gb = [rot(), rot()]
            for hd in range(NHD):
                s = sq2[:, hd % 2, 0:TQ]
                S.act(s, o_f[:, hd, :], AF.Square)
                S.mm(gb[hd // 4][:, (hd % 4) * 128:(hd % 4 + 1) * 128], C.ones_b[:], s, start=True, stop=True)
            rs2 = [rstd, tmp]
            for q in range(2):
                S.ts("dve", rs2[q][:], gb[q][:], 1.0 / 128.0, EPS, ALU.mult, ALU.add)
                S.act(rs2[q][:], rs2[q][:], AF.Sqrt)
                S.recip(rs2[q][:], rs2[q][:])
            for hd in range(NHD):
                S.stt(yt[:], o_f[:, hd, :], nw[:, 0:1], rs2[hd // 4][:, (hd % 4) * 128:(hd % 4 + 1) * 128], ALU.mult, ALU.mult)
                S.tt("dve", yn.p(hd)[:, hd, :], yt[:], sg[:, hd * TQ:(hd + 1) * TQ], ALU.mult)
            for m4 in range(2):
                bk = rot()
                for u in range(4):
                    m = m4 * 4 + u
                    for c in range(NHD):
                        S.mm(bk[:, u * 128:(u + 1) * 128], wout[:, c, m * 128:(m + 1) * 128], yn.p(c)[:, c, :],
                             start=(c == 0), stop=(c == NHD - 1))
                for u in range(4):
                    m = m4 * 4 + u
                    S.tt("dve", xb[:, m, :], xb[:, m, :], bk[:, u * 128:(u + 1) * 128], ALU.add)
            S.dma("sp", x_dst.v(xdv[:, :, t0:t0 + TQ], i), xb[:])
    S.barrier()


def make_consts3():
    c = np.zeros((128, 900), np.float32)
    t = np.arange(128)
    s = t[:, None]
    l = t[None, :]
    same = (s // 64 == l // 64)
    c[:, 0:128] = (same & (s <= l))
    c[:, 128:256] = same
    c[:, 256:384] = (s < 64) & (l >= 0)
    c[:, 384:512] = (s >= 64) & (l >= 0)
    c[:, 512:640] = np.where(same & (s <= l), 0.0, -1e6)
    c[:, 640:768] = np.where(same & (s < l), 0.0, -1e6)
    c[:, 768:896] = np.where(same & (l < s), 0.0, -1e6)
    c[:, 896] = (t < 64)
    c[:, 897] = (t >= 64)
    return c


def gdn_layer(C, x_src, x_dst, ln_d, inw_d, cw_d, alog_d, dtb_d, nw_d, ow_d, consts3_d, NT, SEQ):
    S = C.S
    TQ = 128
    NV = 16
    NQ = 8
    QS = 128 ** -0.5
    with ExitStack() as ls:
        win = S.sbuf(ls, "g_win", [128, KD, 6176], BF16)
        wout = S.sbuf(ls, "g_wout", [128, 16, D], BF16)
        lnw = S.sbuf(ls, "g_lnw", [128, KD], F32)
        cw = S.sbuf(ls, "g_cw", [128, 4, 32], F32)
        nw = S.sbuf(ls, "g_nw", [128, 1], F32)
        dtb_r = S.sbuf(ls, "g_dtb", [128, NV], F32)
        nea_r = S.sbuf(ls, "g_nea", [128, NV], F32)
        c3 = S.sbuf(ls, "g_c3", [128, 900], F32)
        halo = S.sbuf(ls, "g_halo", [128, 32, 3], F32)
        xbs = [S.sbuf(ls, f"g_xb{i}", [128, KD, TQ], F32) for i in range(1)]
        sq2 = S.sbuf(ls, "g_sq", [128, 2, TQ], BF16)
        rstd = S.sbuf(ls, "g_rstd", [128, 512], F32)
        tmp = S.sbuf(ls, "g_tmp", [128, 256], F32)
        hn = S.sbuf(ls, "g_hn", [128, KD, TQ], BF16)
        zs = S.sbuf(ls, "g_zs", [128, 16, TQ], BF16)
        gext = [S.sbuf(ls, f"g_gext{i}", [128, TQ + 3], F32) for i in range(2)]
        cv = [S.sbuf(ls, f"g_cv{i}", [128, TQ], F32) for i in range(2)]
        qc = S.sbuf(ls, "g_qc", [128, 8, TQ], F32)
        qn = S.sbuf(ls, "g_qn", [128, 8, TQ], BF16)
        knb = S.sbuf(ls, "g_knb", [128, 8, TQ], BF16)
        k_tm = S.sbuf(ls, "g_ktm", [128, 8, 128], F32)
        vc = S.sbuf(ls, "g_vc", [128, 4, TQ], F32)
        v_tm = S.sbuf(ls, "g_vtm", [128, NV, 128], F32)
        sm = S.sbuf(ls, "g_sm", [128, 20, NV], F32)
        gcT = S.sbuf(ls, "g_gcT", [16, TQ], F32)
        ngcT = S.sbuf(ls, "g_ngcT", [16, TQ], F32)
        gbT = S.sbuf(ls, "g_gbT", [16, TQ], F32)
        state = S.sbuf(ls, "g_state", [128, NV, 128], F32)
        state_b = S.sbuf(ls, "g_stateb", [128, NV, 128], BF16)
        E1 = [S.sbuf(ls, f"g_E1{e}", [128, 128], F32) for e in range(2)]
        Ebc = [S.sbuf(ls, f"g_Ebc{e}", [128, 128], F32) for e in range(2)]
        PM = [[S.sbuf(ls, f"g_PM{e}{i}", [128, 128], F32) for i in range(2)] for e in range(2)]
        PN = [[S.sbuf(ls, f"g_PN{e}{i}", [128, 128], F32) for i in range(2)] for e in range(2)]
        Rm = [S.sbuf(ls, f"g_R{e}", [128, 128], F32) for e in range(2)]
        attT = [S.sbuf(ls, f"g_attT{e}", [128, 128], BF16) for e in range(2)]
        TTb = [S.sbuf(ls, f"g_TTb{e}", [128, 128], BF16) for e in range(2)]
        qdec = [S.sbuf(ls, f"g_qdec{e}", [128, 128], BF16) for e in range(2)]
        kbg = [S.sbuf(ls, f"g_kbg{e}", [128, 128], BF16) for e in range(2)]
        vb = [S.sbuf(ls, f"g_vb{e}", [128, 128], BF16) for e in range(2)]
        kend = [[S.sbuf(ls, f"g_kend{e}{c}", [128, 128], BF16) for c in range(2)] for e in range(2)]
        u_f = [S.sbuf(ls, f"g_uf{e}", [128, 128], F32) for e in range(2)]
        wTb = [S.sbuf(ls, f"g_wTb{e}", [128, 128], BF16) for e in range(2)]
        vnew = [S.sbuf(ls, f"g_vnew{e}", [128, 128], BF16) for e in range(2)]
        S1b = [S.sbuf(ls, f"g_S1b{e}", [128, 128], BF16) for e in range(2)]
        o_f = [S.sbuf(ls, f"g_of{e}", [128, 128], F32) for e in range(2)]
        yt = S.sbuf(ls, "g_yt", [128, TQ], F32)
        yn = S.sbuf(ls, "g_yn", [128, NV, TQ], BF16)

        (b_, beta, lnbeta, ar, t1, t2, sp, g_, gc, gb, tot, egl0, egl1, ke, ke0, ke1, bg, t3) = [sm[:, i, :] for i in range(18)]

        load_colvec(S, "sp", lnw, ln_d, KD)
        S.dma("sp", nw[:], nw_d.v(nw_d.ap.rearrange("(p o) -> p o", o=1)))
        S.dma("sp", cw[:], cw_d.v(cw_d.ap.rearrange("w (j p) -> p w j", p=128)), allow_slow_non_contiguous=True)
        S.dma("sp", dtb_r[:], dtb_d.v(dtb_d.ap.partition_broadcast(128)))
        S.dma("sp", nea_r[:], alog_d.v(alog_d.ap.partition_broadcast(128)))
        S.dma("sp", c3[:], consts3_d.v(consts3_d.ap[:, :]))
        S.act(nea_r[:], nea_r[:], AF.Exp)
        S.ts("dve", nea_r[:], nea_r[:], -1.0, None, ALU.mult)
        for e in range(2):
            S.memset("pool", vnew[e][:], 0.0)
        inv = inw_d.ap.rearrange("(k p) n -> p k n", p=128)
        for k in range(KD):
            S.dma("pool", win.p(k)[:, k, :], inw_d.v(inv[:, k, :]))
        owv = ow_d.ap.rearrange("(c p) n -> p c n", p=128)
        for c0 in range(0, 16, 8):
            S.dma("pool", wout.p(c0)[:, c0:c0 + 8, :], ow_d.v(owv[:, c0:c0 + 8, :]))
        tri64 = c3[:, 0:128]
        bd = c3[:, 128:256]
        cm0 = c3[:, 256:384]
        cm1 = c3[:, 384:512]
        mask_i = c3[:, 512:640]
        mask_u = c3[:, 640:768]
        mask_l = c3[:, 768:896]

        ntile = NT // TQ
        xsv = x_src.ap.rearrange("(k p) t -> p k t", p=128)
        xdv = x_dst.ap.rearrange("(k p) t -> p k t", p=128)
        bankA = C.banks[0]
        bankP = C.banks[1:3]
        bankQ = C.banks[3:5]
        rot = Rot(C.banks[5:8])

        def ind(h):
            return V(C.cf.t[0:16, h:h + 1].to_broadcast([16, 128]), C.cf._buf(0))

        def l2norm_group(is_q):
            src = qc
            for q4 in range(2):
                nb = rot()
                for u in range(4):
                    hq = q4 * 4 + u
                    s = sq2[:, hq % 2, 0:TQ]
                    S.act(s, src.p(hq)[:, hq, :], AF.Square)
                    S.mm(nb[:, u * 128:(u + 1) * 128], C.ones_b[:], s, start=True, stop=True)
                S.ts("dve", rstd[:], nb[:], 1.0, EPS, ALU.mult, ALU.add)
                S.act(rstd[:], rstd[:], AF.Sqrt)
                S.recip(rstd[:], rstd[:])
                for u in range(4):
                    hq = q4 * 4 + u
                    if is_q:
                        S.stt(qn.p(hq)[:, hq, :], src.p(hq)[:, hq, :], QS, rstd[:, u * 128:(u + 1) * 128], ALU.mult, ALU.mult)
                    else:
                        S.tt("dve", src.p(hq)[:, hq, :], src.p(hq)[:, hq, :], rstd[:, u * 128:(u + 1) * 128], ALU.mult)
                        S.copy("pool", knb.p(hq)[:, hq, :], src.p(hq)[:, hq, :])
            if not is_q:
                for q4 in range(2):
                    tb_ = rot()
                    for u in range(4):
                        hq = q4 * 4 + u
                        S.transpose(tb_[:, u * 128:(u + 1) * 128], src.p(hq)[:, hq, :], C.idf)
                    S.copy("act", V(k_tm.t[:, q4 * 4:(q4 + 1) * 4, :].rearrange("p a b -> p (a b)"), k_tm._buf(0)), tb_[:])

        for i in range(ntile):
            t0 = i * TQ
            xb = xbs[0]
            S.dma("sp", xb[:], x_src.v(xsv[:, :, t0:t0 + TQ], i))
            if t0 % SEQ == 0:
                S.memset("pool", halo[:], 0.0)
                S.memset("pool", state[:], 0.0)
                S.memset("pool", state_b[:], 0.0)
            rms_block(C, xb, lnw, hn, sq2, rstd, tmp, nt=TQ)
            bk = rot()
            for k in range(KD):
                S.mm(bk[:, 0:32], hn[:, k, :], win.p(k)[:, k, 6144:6176], start=(k == 0), stop=(k == KD - 1))
            S.copy("act", b_, bk[:, 0:16])
            S.tt("dve", ar, bk[:, 16:32], dtb_r[:], ALU.add)
            S.act(beta, b_, AF.Sigmoid)
            S.act(lnbeta, beta, AF.Ln)
            softplus_small(S, sp, ar, t1, t2)
            S.tt("dve", g_, sp, nea_r[:], ALU.mult)
            bk = rot()
            S.mm(bk[:, 0:16], tri64, g_, start=True, stop=True)
            S.mm(bk[0:16, 128:256], g_, tri64, start=True, stop=True)
            S.mm(bk[:, 256:272], bd, g_, start=True, stop=True)
            S.mm(bk[:, 272:288], cm0, g_, start=True, stop=True)
            S.mm(bk[:, 288:304], cm1, g_, start=True, stop=True)
            S.copy("act", gc, bk[:, 0:16])
            S.copy("act", gcT[:], bk[0:16, 128:256])
            S.act(ngcT[:], bk[0:16, 128:256], AF.Copy, scale=-1.0)
            S.act(egl0, bk[:, 272:288], AF.Exp)
            S.act(egl1, bk[:, 288:304], AF.Exp)
            S.tt("dve", t3, bk[:, 256:272], gc, ALU.subtract)
            S.act(ke, t3, AF.Exp)
            S.ts("dve", ke0, ke, c3[:, 896:897], None, ALU.mult)
            S.ts("dve", ke1, ke, c3[:, 897:898], None, ALU.mult)
            S.act(t3, gc, AF.Exp)
            S.tt("dve", bg, t3, beta, ALU.mult)
            S.tt("dve", gb, gc, lnbeta, ALU.add)
            bk = rot()
            S.mm(bk[0:16, 0:128], gb, C.idf, start=True, stop=True)
            S.copy("act", gbT[:], bk[0:16, 0:128])
            for q4 in range(12):
                bk = rot()
                for u in range(4):
                    ct = q4 * 4 + u
                    for k in range(KD):
                        S.mm(bk[:, u * 128:(u + 1) * 128], win.p(k)[:, k, ct * 128:(ct + 1) * 128], hn[:, k, :],
                             start=(k == 0), stop=(k == KD - 1))
                if q4 >= 8:
                    z4 = q4 - 8
                    S.act(zs[:, z4 * 4:(z4 + 1) * 4, :], V(bk.t[:, :].rearrange("p (a b) -> p a b", a=4), bk._buf(0)), AF.Silu)
                    continue
                for u in range(4):
                    j = q4 * 4 + u
                    g = gext[j % 2]
                    S.copy("pool", g[:, 0:3], halo[:, j, :])
                    S.copy("act", g[:, 3:TQ + 3], bk[:, u * 128:(u + 1) * 128])
                    S.copy("pool", halo[:, j, :], g[:, TQ:TQ + 3])
                    c = cv[j % 2]
                    S.ts("dve", c[:], g[:, 0:TQ], cw[:, 0, j:j + 1], None, ALU.mult)
                    for w in range(1, 4):
                        S.stt(c[:], g[:, w:TQ + w], cw[:, w, j:j + 1], c[:], ALU.mult, ALU.add)
                    if j < 16:
                        S.act(qc.p(j % 8)[:, j % 8, :], c[:], AF.Silu)
                        if j % 8 == 7:
                            l2norm_group(j < 8)
                    else:
                        jv = j - 16
                        S.act(vc.p(jv % 4)[:, jv % 4, :], c[:], AF.Silu)
                        if jv % 4 == 3:
                            tb_ = rot()
                            for u2 in range(4):
                                j2 = jv - 3 + u2
                                S.transpose(tb_[:, u2 * 128:(u2 + 1) * 128], vc.p(j2 % 4)[:, j2 % 4, :], C.idf)
                            S.copy("act", V(v_tm.t[:, jv - 3:jv + 1, :].rearrange("p a b -> p (a b)"), v_tm._buf(0)), tb_[:])
            for p in range(NQ):
                hq = p
                S.mm(bankA[:, 0:128], knb.p(hq)[:, hq, :], knb.p(hq)[:, hq, :], start=True, stop=True)
                S.mm(bankA[:, 128:256], knb.p(hq)[:, hq, :], qn.p(hq)[:, hq, :], start=True, stop=True)
                cur = [0, 0]
                for e in range(2):
                    hv = 2 * p + e
                    P = bankP[e]
                    S.mm(P[:, 0:128], ind(hv), gcT[:], start=True, stop=True)
                    S.mm(P[:, 128:256], ind(hv), gbT[:], start=True, stop=True)
                    S.mm(P[:, 256:384], ind(hv), ngcT[:], start=True, stop=True)
                    pm, pn = PM[e][0], PN[e][0]
                    S.stt(E1[e][:], P[:, 0:128], gc[:, hv:hv + 1], mask_i, ALU.subtract, ALU.add)
                    S.act(E1[e][:], E1[e][:], AF.Exp)
                    S.stt(pn[:], P[:, 128:256], gc[:, hv:hv + 1], mask_u, ALU.subtract, ALU.add)
                    S.act(pn[:], pn[:], AF.Exp)
                    S.stt(pm[:], P[:, 256:384], gb[:, hv:hv + 1], mask_l, ALU.add, ALU.add)
                    S.act(pm[:], pm[:], AF.Exp)
                    S.act(Ebc[e][:], P[:, 0:128], AF.Exp)
                    S.tt("dve", attT[e][:], E1[e][:], bankA[:, 128:256], ALU.mult)
                    S.tt("dve", pn[:], pn[:], bankA[:, 0:128], ALU.mult)
                    S.tt("dve", pm[:], pm[:], bankA[:, 0:128], ALU.mult)
                    S.stt(Rm[e][:], pn[:], -1.0, C.idf, ALU.mult, ALU.add)
                    S.tt("pool", qdec[e][:], qn.p(hq)[:, hq, :], Ebc[e][:], ALU.mult)
                    S.act(kbg[e][:], k_tm[:, hq, :], AF.Copy, scale=bg[:, hv:hv + 1])
                    S.act(vb[e][:], v_tm[:, hv, :], AF.Copy, scale=beta[:, hv:hv + 1])
                    S.act(kend[e][0][:], k_tm[:, hq, :], AF.Copy, scale=ke0[:, hv:hv + 1])
                    S.act(kend[e][1][:], k_tm[:, hq, :], AF.Copy, scale=ke1[:, hv:hv + 1])
                for lev in range(1, 6):
                    for e in range(2):
                        P = bankP[e]
                        a = cur[e]
                        pm, pn = PM[e][a], PN[e][a]
                        pm2, pn2 = PM[e][1 - a], PN[e][1 - a]
                        S.mm(P[:, 0:128], pn[:], pm[:], start=True, stop=True)
                        if lev < 5:
                            S.mm(P[:, 128:256], pm[:], pn[:], start=True, stop=True)
                        S.copy("act", pm2[:], P[:, 0:128])
                        if lev < 5:
                            S.copy("act", pn2[:], P[:, 128:256])
                        S.mm(P[:, 256:384], pm2[:], Rm[e][:], start=True, stop=True)
                        S.tt("dve", Rm[e][:], Rm[e][:], P[:, 256:384], ALU.add)
                        cur[e] = 1 - a
                for e in range(2):
                    hv = 2 * p + e
                    Q = bankQ[e]
                    P = bankP[e]
                    S.copy("act", TTb[e][:], Rm[e][:])
                    S.mm(Q[:, 0:128], TTb[e][:], vb[e][:], start=True, stop=True)
                    S.mm(Q[:, 128:256], kbg[e][:], TTb[e][:], start=True, stop=True)
                    S.copy("act", u_f[e][:], Q[:, 0:128])
                    S.copy("act", wTb[e][:], Q[:, 128:256])
                    S.mm(Q[0:64, 256:384], wTb[e][:, 0:64], state_b[:, hv, :], start=True, stop=True)
                    S.tt("dve", vnew[e][0:64, :], u_f[e][0:64, :], Q[0:64, 256:384], ALU.subtract)
                    S.mm(Q[:, 384:512], kend[e][0][:], vnew[e][:], start=True, stop=True)
                    S.stt(state[:, hv, :], state[:, hv, :], egl0[:, hv:hv + 1], Q[:, 384:512], ALU.mult, ALU.add)
                    S.copy("act", S1b[e][:], state[:, hv, :])
                    S.mm(Q[64:128, 256:384], wTb[e][:, 64:128], S1b[e][:], start=True, stop=True)
                    S.tt("dve", vnew[e][64:128, :], u_f[e][64:128, :], Q[64:128, 256:384], ALU.subtract)
                    S.mm(Q[:, 384:512], kend[e][1][:], vnew[e][:], start=True, stop=True)
                    S.stt(state[:, hv, :], state[:, hv, :], egl1[:, hv:hv + 1], Q[:, 384:512], ALU.mult, ALU.add)
                    S.mm(P[:, 0:128], vnew[e][:], attT[e][:], start=True, stop=False)
                    S.mm(P[:, 0:64], state_b[:, hv, :], qdec[e][:, 0:64], start=False, stop=False)
                    S.mm(P[:, 64:128], S1b[e][:], qdec[e][:, 64:128], start=False, stop=True)
                    S.copy("act", state_b[:, hv, :], state[:, hv, :])
                    S.copy("act", o_f[e][:], P[:, 0:128])
                for e in range(2):
                    s = sq2[:, e, 0:TQ]
                    S.act(s, o_f[e][:], AF.Square)
                    S.mm(bankA[:, 256 + e * 128:256 + (e + 1) * 128], C.ones_b[:], s, start=True, stop=True)
                S.ts("dve", tmp[:, 0:256], bankA[:, 256:512], 1.0 / 128.0, EPS, ALU.mult, ALU.add)
                S.act(tmp[:, 0:256], tmp[:, 0:256], AF.Sqrt)
                S.recip(tmp[:, 0:256], tmp[:, 0:256])
                for e in range(2):
                    hv = 2 * p + e
                    S.stt(yt[:], o_f[e][:], nw[:, 0:1], tmp[:, e * 128:(e + 1) * 128], ALU.mult, ALU.mult)
                    S.tt("dve", yn.p(hv)[:, hv, :], yt[:], zs[:, hv, :], ALU.mult)
            for m4 in range(2):
                bk = rot()
                for u in range(4):
                    m = m4 * 4 + u
                    for c in range(16):
                        S.mm(bk[:, u * 128:(u + 1) * 128], wout.p((c // 8) * 8)[:, c, m * 128:(m + 1) * 128], yn.p(c)[:, c, :],
                             start=(c == 0), stop=(c == 15))
                for u in range(4):
                    m = m4 * 4 + u
                    S.tt("dve", xb[:, m, :], xb[:, m, :], bk[:, u * 128:(u + 1) * 128], ALU.add)
            S.dma("sp", x_dst.v(xdv[:, :, t0:t0 + TQ], i), xb[:])
    S.barrier()


def final_norm_layer(C, x_src, out_d, ln_d, NT):
    S = C.S
    TB = 512
    with ExitStack() as ls:
        lnw = S.sbuf(ls, "n_lnw", [128, KD], F32)
        xbs = [S.sbuf(ls, f"n_xb{i}", [128, KD, TB], F32) for i in range(2)]
        sq2 = S.sbuf(ls, "n_sq", [128, 2, TB], BF16)
        rstd = S.sbuf(ls, "n_rstd", [128, TB], F32)
        tmp = S.sbuf(ls, "n_tmp", [128, TB], F32)
        load_colvec(S, "sp", lnw, ln_d, KD)
        nblk = NT // TB
        xsv = x_src.ap.rearrange("(k p) t -> p k t", p=128)
        xdv = out_d.ap.rearrange("(k p) t -> p k t", p=128)
        for i in range(nblk):
            xb = xbs[i % 2]
            S.dma("sp", xb[:], x_src.v(xsv[:, :, i * TB:(i + 1) * TB], i))
            rms_block(C, xb, lnw, xb, sq2, rstd, tmp, nt=TB)
            S.dma("sp", out_d.v(xdv[:, :, i * TB:(i + 1) * TB], i), xb[:])
    S.barrier()


DEPTH = 4
NSEQ_CORE = 2
SEQ_LEN = 2048
NMEM = 256
N_CORES = 8

W_SHAPES = {
    "ln_mix": (4, 1024), "ln_xattn": (4, 1024), "ln_mem": (4, 1024), "ln_ffn": (4, 1024), "final_norm": (1024,),
    "m_in_w": (2, 1024, 6176), "m_conv_w": (2, 4, 4096), "m_conv_b": (2, 4096), "m_dt_bias": (2, 32), "m_a_log": (2, 32),
    "m_d": (2, 32), "m_norm_w": (2, 2048), "m_out_w": (2, 2048, 1024),
    "h_in_w": (1, 1024, 4096), "h_lower_bounds": (4, 1024), "h_norm_w": (1, 128), "h_out_w": (1, 1024, 1024),
    "g_in_w": (1, 1024, 6176), "g_conv_w": (1, 4, 4096), "g_a_log": (1, 16), "g_dt_bias": (1, 16), "g_norm_w": (1, 128),
    "g_out_w": (1, 2048, 1024),
    "xa_q": (4, 1024, 1024), "xa_kv": (4, 1024, 2048), "xa_o": (4, 1024, 1024),
    "f_up": (4, 1024, 5632), "f_conv_w": (4, 3, 2816), "f_conv_b": (4, 2816), "f_down": (4, 2816, 1024),
}


def build_program(nseq=NSEQ_CORE, seq=SEQ_LEN, depth=DEPTH):
    NT = nseq * seq
    nc = bass.Bass("TRN2", target_bir_lowering=False)

    def din(name, shape):
        return Dram(nc.dram_tensor(name, list(shape), F32, kind="ExternalInput").ap(), name)

    xT = din("xT", [D, NT])
    memT = din("memT", [D, nseq * NMEM])
    consts = din("consts", [128, 512])
    consts2 = din("consts2", [128, 260])
    consts3 = din("consts3", [128, 900])
    W = {k: din(k, s) for k, s in W_SHAPES.items()}
    outT = Dram(nc.dram_tensor("outT", [D, NT], F32, kind="ExternalOutput").ap(), "outT")
    res = Dram(nc.dram_tensor("res", [D, NT], F32).ap(), "res")

    def sub(name, idx):
        return Dram(W[name].ap[idx], f"{name}[{idx}]")

    with ExitStack() as gs:
        S = Sched(nc, gs)
        C = Ctx(S, gs, consts)
        load_consts(C, gs, consts)
        S.barrier()
        ia = ib = ic = 0
        src = xT
        for i in range(depth):
            if i % 3 == 0:
                ssd_layer(C, src, res, sub("ln_mix", i), sub("m_in_w", ia), sub("m_conv_w", ia), sub("m_conv_b", ia),
                          sub("m_dt_bias", ia), sub("m_a_log", ia), sub("m_d", ia), sub("m_norm_w", ia), sub("m_out_w", ia), NT, seq)
                ia += 1
            elif i % 3 == 1:
                hgrn_layer(C, src, res, sub("ln_mix", i), sub("h_in_w", ib), W["h_lower_bounds"], sub("h_norm_w", ib),
                           sub("h_out_w", ib), consts2, i, depth, NT, seq)
                ib += 1
            else:
                gdn_layer(C, src, res, sub("ln_mix", i), sub("g_in_w", ic), sub("g_conv_w", ic), sub("g_a_log", ic),
                          sub("g_dt_bias", ic), sub("g_norm_w", ic), sub("g_out_w", ic), consts3, NT, seq)
                ic += 1
            src = res
            xattn_layer(C, res, res, memT, sub("ln_xattn", i), sub("ln_mem", i), sub("xa_q", i), sub("xa_kv", i), sub("xa_o", i), NT, seq)
            ffn_layer(C, res, res, sub("ln_ffn", i), sub("f_up", i), sub("f_conv_w", i), sub("f_conv_b", i), sub("f_down", i), NT, seq)
        final_norm_layer(C, res, outT, W["final_norm"], NT)
        S.finish()
        S.emit()
    return nc


_NC_CACHE = {}


def kernel(**inputs):
    x = np.asarray(inputs["x"], dtype=np.float32)
    mem = np.asarray(inputs["mem"], dtype=np.float32)
    B, L, _ = x.shape
    nseq = B // N_CORES
    if "nc" not in _NC_CACHE:
        _NC_CACHE["nc"] = build_program(nseq, L, DEPTH)
    nc = _NC_CACHE["nc"]
    shared = {k: np.ascontiguousarray(np.asarray(inputs[k], dtype=np.float32)) for k in W_SHAPES}
    shared["consts"] = make_consts()
    shared["consts2"] = make_consts2()
    shared["consts3"] = make_consts3()
    in_maps = []
    for c in range(N_CORES):
        m = dict(shared)
        m["xT"] = np.ascontiguousarray(x[c * nseq:(c + 1) * nseq].reshape(nseq * L, D).T)
        m["memT"] = np.ascontiguousarray(mem[c * nseq:(c + 1) * nseq].reshape(nseq * NMEM, D).T)
        in_maps.append(m)
    res = run_bass_kernel_spmd(nc, in_maps, core_ids=list(range(N_CORES)))
    out = np.empty((B, L, D), np.float32)
    for c in range(N_CORES):
        out[c * nseq:(c + 1) * nseq] = res.results[c]["outT"].T.reshape(nseq, L, D)
    return out
```
